# Optimizing a Trainium2 kernel written in Bass

```python
import math
import jax
import jax.numpy as jnp
from jax import lax
import numpy as np

D_MODEL = 1024
BATCH = 32
SEQ = 2048
DEPTH = 2

GRID_W = 64
CTX_LEN = 256
EPS = 1e-6
N_MOD = 6
S5_WIDTH = D_MODEL // 4
S5_GROUP = 16
S5_GROUPS = S5_WIDTH // S5_GROUP
S5_STATE = 64
S5_DT_MIN = 1e-3
S5_DT_MAX = 1e-1
RET_HEADS = 4
RET_WIDTH = 3 * D_MODEL // 8
RET_DV = RET_WIDTH // RET_HEADS
RET_DK = RET_DV // 2
RET_CHUNK = 128
ROPE_BASE = 10000.0
GLA_HEADS = 4
GLA_WIDTH = 3 * D_MODEL // 8
GLA_DV = GLA_WIDTH // GLA_HEADS
GLA_DK = GLA_DV // 2
GLA_RANK = 16
GLA_TAU = 16.0
GLA_CHUNK = 64
FFN_HIDDEN = -(-8 * D_MODEL // (3 * 256)) * 256
IN_SIZES = (S5_WIDTH,
            RET_HEADS * RET_DK, RET_HEADS * RET_DK, RET_WIDTH, RET_WIDTH,
            GLA_HEADS * GLA_DK, GLA_HEADS * GLA_DK, GLA_WIDTH, GLA_WIDTH, GLA_RANK, GLA_RANK,
            D_MODEL, D_MODEL, D_MODEL)
IN_DIM = sum(IN_SIZES)

kernel_name = 'hybrid_s5_retnet_gla_prefix_dit'


def _rms_norm(x, gain):
    xf = x.astype(jnp.float32)
    y = xf * lax.rsqrt(jnp.mean(xf * xf, axis=-1, keepdims=True) + EPS)
    return (y * gain.astype(jnp.float32)).astype(x.dtype)


def _modulate(h, shift, scale):
    return h * (1.0 + scale) + shift


def _split_in(p):
    return jnp.split(p, np.cumsum(IN_SIZES)[:-1].tolist(), axis=-1)


def _heads(t, n_heads):
    b, l, _ = t.shape
    return t.reshape(b, l, n_heads, -1).transpose(0, 2, 1, 3)


def _merge_heads(t):
    b, h, l, d = t.shape
    return t.transpose(0, 2, 1, 3).reshape(b, l, h * d)


def _flip(t, rev, axis=2):
    return jnp.flip(t, axis) if rev else t


def _to_chunks(t, size):
    b, h, l, d = t.shape
    return jnp.moveaxis(t.reshape(b, h, l // size, size, d), 2, 0)


def _from_chunks(t):
    n, b, h, c, d = t.shape
    return jnp.moveaxis(t, 0, 2).reshape(b, h, n * c, d)


def _group_norm(o):
    mu = jnp.mean(o, axis=-1, keepdims=True)
    var = jnp.mean(jnp.square(o - mu), axis=-1, keepdims=True)
    return (o - mu) * lax.rsqrt(var + EPS)


def _head_rms(o):
    return o * lax.rsqrt(jnp.mean(o * o, axis=-1, keepdims=True) + EPS)


def _rope_2d(t, rows):
    nf = t.shape[-1] // 4
    inv = 1.0 / (ROPE_BASE ** (jnp.arange(nf, dtype=jnp.float32) / nf))
    r = jnp.repeat(jnp.arange(rows, dtype=jnp.float32), GRID_W)
    col = jnp.tile(jnp.arange(GRID_W, dtype=jnp.float32), rows)
    ang = jnp.concatenate([r[:, None] * inv, col[:, None] * inv], axis=-1)
    cos, sin = jnp.cos(ang), jnp.sin(ang)
    t1, t2 = t[..., 0::2], t[..., 1::2]
    return jnp.stack([t1 * cos - t2 * sin, t1 * sin + t2 * cos], axis=-1).reshape(t.shape)


def _s5_discretize(lam_re, lam_im, log_dt, b_re, b_im):
    dt = jnp.exp(log_dt)[:, None]
    mag = jnp.exp(lam_re * dt)
    a_re, a_im = mag * jnp.cos(lam_im * dt), mag * jnp.sin(lam_im * dt)
    den = lam_re * lam_re + lam_im * lam_im
    f_re = ((a_re - 1.0) * lam_re + a_im * lam_im) / den
    f_im = (a_im * lam_re - (a_re - 1.0) * lam_im) / den
    bb_re = f_re[..., None] * b_re - f_im[..., None] * b_im
    bb_im = f_re[..., None] * b_im + f_im[..., None] * b_re
    return a_re, a_im, bb_re, bb_im


def _s5_scan(x_re, x_im, a_re, a_im, h0_re, h0_im):
    l = x_re.shape[1]
    ar = jnp.broadcast_to(a_re, (1, l) + a_re.shape)
    ai = jnp.broadcast_to(a_im, (1, l) + a_im.shape)

    def combine(e1, e2):
        a1r, a1i, x1r, x1i = e1
        a2r, a2i, x2r, x2i = e2
        return (a2r * a1r - a2i * a1i, a2r * a1i + a2i * a1r,
                a2r * x1r - a2i * x1i + x2r, a2r * x1i + a2i * x1r + x2i)

    pr, pi, hr, hi = lax.associative_scan(combine, (ar, ai, x_re, x_im), axis=1)
    h0r, h0i = h0_re[:, None], h0_im[:, None]
    return hr + pr * h0r - pi * h0i, hi + pr * h0i + pi * h0r


def _s5_branch(u_ctx, u_lat, lam_re, lam_im, log_dt, b_re, b_im, c_re, c_im, d_skip, glu_w, glu_b, need_ctx):
    f32 = jnp.float32
    lam_re, lam_im, log_dt, b_re, b_im, c_re, c_im, d_skip, glu_w, glu_b = (
        t.astype(f32) for t in (lam_re, lam_im, log_dt, b_re, b_im, c_re, c_im, d_skip, glu_w, glu_b))
    u_c, u_l = u_ctx.astype(f32), u_lat.astype(f32)
    bsz = u_l.shape[0]
    grp_c = u_c.reshape(bsz, u_c.shape[1], S5_GROUPS, S5_GROUP)
    grp_l = u_l.reshape(bsz, u_l.shape[1], S5_GROUPS, S5_GROUP)
    zeros = jnp.zeros((bsz, S5_GROUPS, S5_STATE), f32)
    hc_re = hc_im = hl_re = hl_im = 0.0
    for d in range(2):
        rev = d == 1
        a_re, a_im, bb_re, bb_im = _s5_discretize(lam_re[d], lam_im[d], log_dt[d], b_re, b_im)
        xc = [_flip(jnp.einsum('blgc,gpc->blgp', grp_c, bb), rev, 1) for bb in (bb_re, bb_im)]
        xl = [_flip(jnp.einsum('blgc,gpc->blgp', grp_l, bb), rev, 1) for bb in (bb_re, bb_im)]
        sc_re, sc_im = _s5_scan(xc[0], xc[1], a_re, a_im, zeros, zeros)
        sl_re, sl_im = _s5_scan(xl[0], xl[1], a_re, a_im, sc_re[:, -1], sc_im[:, -1])
        hc_re, hc_im = hc_re + _flip(sc_re, rev, 1), hc_im + _flip(sc_im, rev, 1)
        hl_re, hl_im = hl_re + _flip(sl_re, rev, 1), hl_im + _flip(sl_im, rev, 1)

    def readout(h_re, h_im, u):
        y = jnp.einsum('blgp,gcp->blgc', h_re, c_re) - jnp.einsum('blgp,gcp->blgc', h_im, c_im)
        y = jax.nn.gelu(y.reshape(u.shape) + d_skip * u)
        return y * jax.nn.sigmoid(y @ glu_w + glu_b)

    y_l = readout(hl_re, hl_im, u_l).astype(u_lat.dtype)
    y_c = readout(hc_re, hc_im, u_c).astype(u_ctx.dtype) if need_ctx else None
    return y_c, y_l


def _retention_chunked(q, k, v, log_gamma, s0):
    pos = jnp.arange(RET_CHUNK, dtype=jnp.float32)
    rel = pos[:, None] - pos[None, :]
    intra = jnp.where(rel >= 0, jnp.exp(jnp.maximum(rel, 0.0) * log_gamma[:, None, None]), 0.0)
    q_dec = jnp.exp((pos + 1.0) * log_gamma[:, None])[..., None]
    k_dec = jnp.exp((RET_CHUNK - 1.0 - pos) * log_gamma[:, None])[..., None]
    c_dec = jnp.exp(RET_CHUNK * log_gamma)[:, None, None]

    def step(s, blk):
        qc, kc, vc = blk
        att = jnp.einsum('bhid,bhjd->bhij', qc, kc) * intra
        o = jnp.einsum('bhij,bhjv->bhiv', att, vc) + jnp.einsum('bhid,bhdv->bhiv', qc * q_dec, s)
        s = c_dec * s + jnp.einsum('bhjd,bhjv->bhdv', kc * k_dec, vc)
        return s, o

    s_fin, o = lax.scan(step, s0, (_to_chunks(q, RET_CHUNK), _to_chunks(k, RET_CHUNK), _to_chunks(v, RET_CHUNK)))
    return _from_chunks(o), s_fin


def _retnet_branch(p_ctx, p_lat, log_decay, gn_gain, rows, need_ctx):
    f32 = jnp.float32
    log_decay, gn_gain = log_decay.astype(f32), gn_gain.astype(f32)

    def prep(p, rotate):
        q, k, v, g = (t.astype(f32) for t in p)
        q, k, v = _heads(q, RET_HEADS), _heads(k, RET_HEADS), _heads(v, RET_HEADS)
        if rotate:
            q, k = _rope_2d(q, rows), _rope_2d(k, rows)
        return q, k * RET_DK ** -0.5, v, g

    qc, kc, vc, gc = prep(p_ctx, False)
    ql, kl, vl, gl = prep(p_lat, True)
    zeros = jnp.zeros((ql.shape[0], RET_HEADS, RET_DK, RET_DV), f32)
    o_c = o_l = 0.0
    for d in range(2):
        rev = d == 1
        oc, s_ctx = _retention_chunked(_flip(qc, rev), _flip(kc, rev), _flip(vc, rev), log_decay[d], zeros)
        ol, _ = _retention_chunked(_flip(ql, rev), _flip(kl, rev), _flip(vl, rev), log_decay[d], s_ctx)
        o_c = o_c + _flip(oc, rev)
        o_l = o_l + _flip(ol, rev)

    def readout(o, g, dtype):
        return (_merge_heads(_group_norm(o)) * gn_gain * jax.nn.silu(g)).astype(dtype)

    y_l = readout(o_l, gl, p_lat[0].dtype)
    y_c = readout(o_c, gc, p_ctx[0].dtype) if need_ctx else None
    return y_c, y_l


def _gla_chunked(q, k, v, log_alpha, s0):
    mask = jnp.tril(jnp.ones((GLA_CHUNK, GLA_CHUNK), dtype=bool))

    def step(s, blk):
        qc, kc, vc, gc = blk
        b = jnp.cumsum(gc, axis=-2)
        b_last = b[..., -1:, :]
        q_t = qc * jnp.exp(b)
        k_t = kc * jnp.exp(-b)
        att = jnp.where(mask, jnp.einsum('bhid,bhjd->bhij', q_t, k_t), 0.0)
        o = jnp.einsum('bhij,bhjv->bhiv', att, vc) + jnp.einsum('bhid,bhdv->bhiv', q_t, s)
        s = jnp.exp(b_last)[..., 0, :, None] * s + jnp.einsum('bhjd,bhjv->bhdv', kc * jnp.exp(b_last - b), vc)
        return s, o

    xs = tuple(_to_chunks(t, GLA_CHUNK) for t in (q, k, v, log_alpha))
    s_fin, o = lax.scan(step, s0, xs)
    return _from_chunks(o), s_fin


def _gla_branch(p_ctx, p_lat, gate_w, gate_b, norm_gain, need_ctx):
    f32 = jnp.float32
    gate_w, gate_b, norm_gain = gate_w.astype(f32), gate_b.astype(f32), norm_gain.astype(f32)

    def prep(p):
        q, k, v, g, z_f, z_b = (t.astype(f32) for t in p)
        la = tuple(_heads(jax.nn.log_sigmoid(z @ gate_w[d] + gate_b[d]) / GLA_TAU, GLA_HEADS)
                   for d, z in enumerate((z_f, z_b)))
        return (_heads(q, GLA_HEADS) * GLA_DK ** -0.5, _heads(k, GLA_HEADS), _heads(v, GLA_HEADS), la, g)

    qc, kc, vc, lac, gc = prep(p_ctx)
    ql, kl, vl, lal, gl = prep(p_lat)
    zeros = jnp.zeros((ql.shape[0], GLA_HEADS, GLA_DK, GLA_DV), f32)
    o_c = o_l = 0.0
    for d in range(2):
        rev = d == 1
        oc, s_ctx = _gla_chunked(_flip(qc, rev), _flip(kc, rev), _flip(vc, rev), _flip(lac[d], rev), zeros)
        ol, _ = _gla_chunked(_flip(ql, rev), _flip(kl, rev), _flip(vl, rev), _flip(lal[d], rev), s_ctx)
        o_c = o_c + _flip(oc, rev)
        o_l = o_l + _flip(ol, rev)

    def readout(o, g, dtype):
        return (_merge_heads(_head_rms(o)) * norm_gain * jax.nn.silu(g)).astype(dtype)

    y_l = readout(o_l, gl, p_lat[0].dtype)
    y_c = readout(o_c, gc, p_ctx[0].dtype) if need_ctx else None
    return y_c, y_l


def _merge_branches(ys, gates, w_brs, w_out):
    m = jax.nn.sigmoid(gates[0]) * (ys[0] @ w_brs[0])
    for y, g, w in zip(ys[1:], gates[1:], w_brs[1:]):
        m = m + jax.nn.sigmoid(g) * (y @ w)
    return m @ w_out


def _swiglu(h, w_in, w_out):
    a, b = jnp.split(h @ w_in, 2, axis=-1)
    return (jax.nn.silu(a) * b) @ w_out


def setup_inputs(seed: int = 0) -> dict:
    key = jax.random.key(seed)
    ks = iter(jax.random.split(key, 40))
    f32 = jnp.float32

    def nrm(shape, scale):
        return scale * jax.random.normal(next(ks), shape, f32)

    G, P = S5_GROUPS, S5_STATE
    hippo_im = jnp.pi * jnp.arange(P, dtype=f32)
    ret_base = jnp.log(1.0 - 2.0 ** (-5.0 - jnp.arange(RET_HEADS, dtype=f32)))
    return dict(
        x=nrm((BATCH, SEQ, D_MODEL), 1.0),
        c=nrm((BATCH, D_MODEL), 1.0),
        ctx=nrm((BATCH, CTX_LEN, D_MODEL), 1.0),
        c_ctx=nrm((D_MODEL,), 1.0),
        w_mod=nrm((DEPTH, D_MODEL, N_MOD * D_MODEL), D_MODEL ** -0.5),
        b_mod=nrm((DEPTH, N_MOD * D_MODEL), 0.01),
        norm_mix=1.0 + nrm((DEPTH, D_MODEL), 0.01),
        norm_ffn=1.0 + nrm((DEPTH, D_MODEL), 0.01),
        w_in=nrm((DEPTH, D_MODEL, IN_DIM), D_MODEL ** -0.5),
        s5_lam_re=-0.5 + nrm((DEPTH, 2, G, P), 0.01),
        s5_lam_im=hippo_im + nrm((DEPTH, 2, G, P), 0.01),
        s5_log_dt=jax.random.uniform(next(ks), (DEPTH, 2, G), f32, math.log(S5_DT_MIN), math.log(S5_DT_MAX)),
        s5_b_re=nrm((DEPTH, G, P, S5_GROUP), (2 * S5_GROUP) ** -0.5),
        s5_b_im=nrm((DEPTH, G, P, S5_GROUP), (2 * S5_GROUP) ** -0.5),
        s5_c_re=nrm((DEPTH, G, S5_GROUP, P), P ** -0.5),
        s5_c_im=nrm((DEPTH, G, S5_GROUP, P), P ** -0.5),
        s5_d=nrm((DEPTH, S5_WIDTH), 1.0),
        s5_glu_w=nrm((DEPTH, S5_WIDTH, S5_WIDTH), S5_WIDTH ** -0.5),
        s5_glu_b=nrm((DEPTH, S5_WIDTH), 0.01),
        ret_log_decay=ret_base * (1.0 + nrm((DEPTH, 2, RET_HEADS), 0.05)),
        ret_gn=1.0 + nrm((DEPTH, RET_WIDTH), 0.01),
        gla_gate_w=nrm((DEPTH, 2, GLA_RANK, GLA_HEADS * GLA_DK), GLA_RANK ** -0.5),
        gla_gate_b=nrm((DEPTH, 2, GLA_HEADS * GLA_DK), 0.1),
        gla_norm=1.0 + nrm((DEPTH, GLA_WIDTH), 0.01),
        w_br_s5=nrm((DEPTH, S5_WIDTH, D_MODEL), S5_WIDTH ** -0.5),
        w_br_ret=nrm((DEPTH, RET_WIDTH, D_MODEL), RET_WIDTH ** -0.5),
        w_br_gla=nrm((DEPTH, GLA_WIDTH, D_MODEL), GLA_WIDTH ** -0.5),
        w_out=nrm((DEPTH, D_MODEL, D_MODEL), D_MODEL ** -0.5),
        w_ffn_in=nrm((DEPTH, D_MODEL, 2 * FFN_HIDDEN), D_MODEL ** -0.5),
        w_ffn_out=nrm((DEPTH, FFN_HIDDEN, D_MODEL), FFN_HIDDEN ** -0.5),
        norm_final=1.0 + nrm((D_MODEL,), 0.01),
    )


def reference(x, c, ctx, c_ctx, w_mod, b_mod, norm_mix, norm_ffn, w_in,
              s5_lam_re, s5_lam_im, s5_log_dt, s5_b_re, s5_b_im, s5_c_re, s5_c_im, s5_d, s5_glu_w, s5_glu_b,
              ret_log_decay, ret_gn, gla_gate_w, gla_gate_b, gla_norm,
              w_br_s5, w_br_ret, w_br_gla, w_out, w_ffn_in, w_ffn_out, norm_final):
    rows = x.shape[1] // GRID_W
    s_c = jax.nn.silu(c)
    s_cc = jax.nn.silu(c_ctx)
    for l in range(DEPTH):
        need_ctx = l < DEPTH - 1
        m_l = jnp.split((s_c @ w_mod[l] + b_mod[l])[:, None, :], N_MOD, axis=-1)
        m_c = jnp.split(s_cc @ w_mod[l] + b_mod[l], N_MOD, axis=-1)
        p_l = _split_in(_modulate(_rms_norm(x, norm_mix[l]), m_l[0], m_l[1]) @ w_in[l])
        p_c = _split_in(_modulate(_rms_norm(ctx, norm_mix[l]), m_c[0], m_c[1]) @ w_in[l])
        s5_c, s5_l = _s5_branch(p_c[0], p_l[0], s5_lam_re[l], s5_lam_im[l], s5_log_dt[l], s5_b_re[l], s5_b_im[l],
                                s5_c_re[l], s5_c_im[l], s5_d[l], s5_glu_w[l], s5_glu_b[l], need_ctx)
        ret_c, ret_l = _retnet_branch(p_c[1:5], p_l[1:5], ret_log_decay[l], ret_gn[l], rows, need_ctx)
        gla_c, gla_l = _gla_branch(p_c[5:11], p_l[5:11], gla_gate_w[l], gla_gate_b[l], gla_norm[l], need_ctx)
        w_brs = (w_br_s5[l], w_br_ret[l], w_br_gla[l])
        x = x + m_l[2] * _merge_branches((s5_l, ret_l, gla_l), p_l[11:14], w_brs, w_out[l])
        x = x + m_l[5] * _swiglu(_modulate(_rms_norm(x, norm_ffn[l]), m_l[3], m_l[4]), w_ffn_in[l], w_ffn_out[l])
        if need_ctx:
            ctx = ctx + m_c[2] * _merge_branches((s5_c, ret_c, gla_c), p_c[11:14], w_brs, w_out[l])
            ctx = ctx + m_c[5] * _swiglu(_modulate(_rms_norm(ctx, norm_ffn[l]), m_c[3], m_c[4]),
                                         w_ffn_in[l], w_ffn_out[l])
    return _rms_norm(x, norm_final)
```

```python
import math
import contextlib
import numpy as np
import concourse.bass as bass
import concourse.mybir as mybir
from concourse.bass_utils import run_bass_kernel_spmd

F32 = mybir.dt.float32
BF16 = mybir.dt.bfloat16
ALU = mybir.AluOpType
AF = mybir.ActivationFunctionType
AX = mybir.AxisListType

D = 1024
SEQ = 2048
CTXL = 256
LT = SEQ + CTXL
DEPTH = 2
NSEQ = 4
INDIM = 5664
FFH = 2816
EPS = 1e-6
NTM = 2560
NDS = 60
DPOOL = {"sync": (0, 24), "gpsimd": (24, 48), "scalar": (48, 60)}
FLAGS = {}

C_ID = 0
C_MF = 128
C_MB = 256
C_MFS = 384
C_MBS = 512
C_IOTA = 640
C_COS = 641
C_SIN = C_COS + 384
C_NV = C_SIN + 384
C_MGF = C_NV + 1280
C_MGB = C_MGF + 128
NCST = C_MGB + 128


class Dep:
    __slots__ = ("w", "r")

    def __init__(self):
        self.w = None
        self.r = []


class Tl:
    def __init__(self, t):
        self.t = t
        self.d = Dep()


class Sched:
    CE = ["tensor", "vector", "scalar", "gpsimd"]
    ENG = ["tensor", "vector", "scalar", "gpsimd", "sync"]

    def __init__(self, nc, stack):
        self.nc = nc
        self.ops = {e: [] for e in self.ENG}
        self.sem = {e: stack.enter_context(nc.semaphore("s_" + e)) for e in self.CE}
        self.cnt = {e: 0 for e in self.CE}
        self.seen = {e: {} for e in self.ENG}
        self.dsem = [stack.enter_context(nc.semaphore("d%d" % i)) for i in range(NDS)]
        self.dcnt = [0] * NDS
        self.dnext = {q: lo for q, (lo, hi) in DPOOL.items()}
        self.nins = 0

    def _semof(self, key):
        return self.sem[key[1]] if key[0] == "e" else self.dsem[key[1]]

    def _collect(self, eng, reads, writes):
        need = {}
        for d in reads:
            if d.w is not None:
                k, v = d.w
                if need.get(k, 0) < v:
                    need[k] = v
        for d in writes:
            if d.w is not None:
                k, v = d.w
                if need.get(k, 0) < v:
                    need[k] = v
            for (k, v) in d.r:
                if need.get(k, 0) < v:
                    need[k] = v
        waits = []
        seen = self.seen[eng]
        for k, v in need.items():
            if eng == "tensor" and k == ("e", "tensor"):
                continue
            if seen.get(k, 0) >= v:
                continue
            seen[k] = v
            waits.append((self._semof(k), v))
        return waits

    def _mark(self, tok, reads, writes):
        ws = set()
        for d in writes:
            d.w = tok
            d.r = []
            ws.add(id(d))
        for d in reads:
            if id(d) not in ws:
                d.r.append(tok)
                if len(d.r) > 48:
                    m = {}
                    for (k, v) in d.r:
                        if m.get(k, 0) < v:
                            m[k] = v
                    d.r = list(m.items())

    def op(self, eng, fn, reads=(), writes=()):
        waits = self._collect(eng, reads, writes)
        self.cnt[eng] += 1
        val = self.cnt[eng]
        sem = self.sem[eng]

        def run(e, fn=fn, waits=waits, sem=sem):
            for (s, v) in waits:
                e.wait_ge(s, v)
            fn(e).then_inc(sem, 1)

        self.ops[eng].append(run)
        self.nins += 1
        self._mark((("e", eng), val), reads, writes)

    def dma(self, q, out, in_, reads=(), writes=(), **kw):
        waits = self._collect(q, reads, writes)
        lo, hi = DPOOL[q]
        i = self.dnext[q]
        self.dnext[q] = lo + (i + 1 - lo) % (hi - lo)
        prev = self.dcnt[i]
        self.dcnt[i] += 16
        val = self.dcnt[i]
        key = ("d", i)
        if prev > 0 and self.seen[q].get(key, 0) < prev:
            waits.append((self.dsem[i], prev))
            self.seen[q][key] = prev
        dsem = self.dsem[i]

        def run(e, waits=waits, dsem=dsem, out=out, in_=in_, kw=kw):
            for (s, v) in waits:
                e.wait_ge(s, v)
            e.dma_start(out=out, in_=in_, **kw).then_inc(dsem, 16)

        self.ops[q].append(run)
        self.nins += 1
        self._mark((key, val), reads, writes)

    def flush(self):
        nc = self.nc
        finals = [(self.dsem[i], self.dcnt[i]) for i in range(NDS) if self.dcnt[i] > 0]
        finals += [(self.sem[e], self.cnt[e]) for e in self.CE if self.cnt[e] > 0]

        def fin(e, finals=finals):
            for (s, v) in finals:
                e.wait_ge(s, v)

        self.ops["sync"].append(fin)
        ops = self.ops
        self.ops = {e: [] for e in self.ENG}
        with nc.Block() as block:
            @block.tensor
            def _(e):
                for f in ops["tensor"]:
                    f(e)

            @block.vector
            def _(e):
                for f in ops["vector"]:
                    f(e)

            @block.scalar
            def _(e):
                for f in ops["scalar"]:
                    f(e)

            @block.gpsimd
            def _(e):
                for f in ops["gpsimd"]:
                    f(e)

            @block.sync
            def _(e):
                for f in ops["sync"]:
                    f(e)
        for e in self.ENG:
            for i in range(NDS):
                self.seen[e][("d", i)] = self.dcnt[i]
            for c in self.CE:
                self.seen[e][("e", c)] = self.cnt[c]


class Rot:
    def __init__(self, items):
        self.items = items
        self.i = 0

    def next(self):
        x = self.items[self.i % len(self.items)]
        self.i += 1
        return x


def bc(ap, shape, axis):
    return ap.unsqueeze(axis).broadcast_to(list(shape))


def build(nseq=NSEQ, depth=DEPTH, debug=False, upto="all"):
    nc = bass.Bass("TRN2", target_bir_lowering=False)

    def din(name, shape, dt=F32):
        return nc.dram_tensor(name, list(shape), dt, kind="ExternalInput").ap()

    skind = "ExternalOutput" if debug else "Internal"

    def scr(name, shape, dt=F32):
        return nc.dram_tensor(name, list(shape), dt, kind=skind).ap()

    xT = din("xT", [nseq, D, LT])
    cT = din("cT", [D, 8])
    cst = din("cst", [128, NCST])
    w_mod = din("w_mod", [DEPTH, D, 6 * D])
    b_mod = din("b_mod", [DEPTH, 6 * D])
    norm_mix = din("norm_mix", [DEPTH, D])
    norm_ffn = din("norm_ffn", [DEPTH, D])
    w_in = din("w_in", [DEPTH, D, INDIM])
    s5lam = din("s5lam", [DEPTH, 64, 2, 32])
    s5dt = din("s5dt", [DEPTH, 32])
    s5B = din("s5B", [DEPTH, 64, 2, 16, 16])
    s5C = din("s5C", [DEPTH, 64, 2, 16, 16])
    s5_d = din("s5_d", [DEPTH, 256])
    s5_glu_w = din("s5_glu_w", [DEPTH, 256, 256])
    s5_glu_b = din("s5_glu_b", [DEPTH, 256])
    ret_ld = din("ret_ld", [DEPTH, 8])
    ret_gn = din("ret_gn", [DEPTH, 384])
    gla_gw = din("gla_gw", [DEPTH, 2, 16, 192])
    gla_gb = din("gla_gb", [DEPTH, 2, 192])
    gla_norm = din("gla_norm", [DEPTH, 384])
    w_br_s5 = din("w_br_s5", [DEPTH, 256, D])
    w_br_ret = din("w_br_ret", [DEPTH, 384, D])
    w_br_gla = din("w_br_gla", [DEPTH, 384, D])
    w_out = din("w_out", [DEPTH, D, D])
    w_ffn_in = din("w_ffn_in", [DEPTH, D, 2 * FFH])
    w_ffn_out = din("w_ffn_out", [DEPTH, FFH, D])
    norm_final = din("norm_final", [D])
    outT = nc.dram_tensor("outT", [nseq, D, SEQ], F32, kind="ExternalOutput").ap()

    TM = scr("TM", [nseq, LT, NTM])
    ZF = scr("ZF", [nseq, 32, LT])
    GATE = scr("GATE", [nseq, 3 * D, LT], BF16)
    OBR = scr("OBR", [nseq, LT, 384])
    OBG = scr("OBG", [nseq, LT, 384])
    YS5 = scr("YS5", [nseq, 256, LT], BF16)
    YRET = scr("YRET", [nseq, 384, LT], BF16)
    YGLA = scr("YGLA", [nseq, 384, LT], BF16)
    XMID = scr("XMID", [nseq, D, LT])
    X1 = scr("X1", [nseq, D, LT])
    HHs = scr("HH", [nseq, FFH, LT], BF16)
    d_HH = [Dep() for _ in range(nseq)]
    d_TM = [Dep() for _ in range(nseq)]
    d_ZF = [Dep() for _ in range(nseq)]
    d_GATE = [Dep() for _ in range(nseq)]
    d_OBR = [Dep() for _ in range(nseq)]
    d_OBG = [Dep() for _ in range(nseq)]
    d_YS5 = [Dep() for _ in range(nseq)]
    d_YRET = [Dep() for _ in range(nseq)]
    d_YGLA = [Dep() for _ in range(nseq)]
    d_XMID = [Dep() for _ in range(nseq)]
    d_X1 = [Dep() for _ in range(nseq)]
    d_OUT = Dep()

    top = contextlib.ExitStack()
    with top:
        S = Sched(nc, top)

        def OP(eng, name, kw, r=(), w=()):
            S.op(eng, lambda e, name=name, kw=kw: getattr(e, name)(**kw), r, w)

        def V(name, kw, r=(), w=()):
            OP("vector", name, kw, r, w)

        def A(name, kw, r=(), w=()):
            OP("scalar", name, kw, r, w)

        def G(name, kw, r=(), w=()):
            OP("gpsimd", name, kw, r, w)

        def PE(name, kw, r=(), w=()):
            OP("tensor", name, kw, r, w)

        def DMA(out, in_, r=(), w=(), q="sync", **kw):
            S.dma(q, out, in_, r, w, **kw)

        uid = [0]

        def mk(stack, name, shape, dt=F32):
            uid[0] += 1
            return Tl(stack.enter_context(nc.sbuf_tensor("%s_%d" % (name, uid[0]), list(shape), dt)))

        def mkp(stack, name, shape, dt=F32):
            uid[0] += 1
            return Tl(stack.enter_context(nc.psum_tensor("%s_%d" % (name, uid[0]), list(shape), dt)))

        CST = mk(top, "CST", [128, NCST])
        MOD = mk(top, "MOD", [128, DEPTH, 48, 8])
        GN12 = mk(top, "GN12", [128, DEPTH, 2, 8, 8])
        NRM = mk(top, "NRM", [128, 3, 2, 8])
        IDB = mk(top, "IDB", [128, 128], BF16)
        ONESB = mk(top, "ONESB", [128, 128], BF16)
        ONES32 = mk(top, "ONES32", [128, 2])
        MASKB = mk(top, "MASKB", [128, 2, 4, 128], BF16)

        DMA(CST.t[:], cst, w=[CST.d])
        V("tensor_copy", dict(out=IDB.t[:], in_=CST.t[:, C_ID:C_ID + 128]), [CST.d], [IDB.d])
        G("memset", dict(ap=ONESB.t[:], constant=1.0 / 1024.0), w=[ONESB.d])
        G("memset", dict(ap=ONES32.t[:], constant=1.0), w=[ONES32.d])
        for dd, co in ((0, C_MF), (1, C_MB)):
            for h in range(4):
                V("tensor_copy", dict(out=MASKB.t[:, dd, h, :], in_=CST.t[:, co:co + 128]), [CST.d], [MASKB.d])
        for l in range(DEPTH):
            DMA(NRM.t[:, 0, l, :], norm_mix[l].rearrange("(k p) -> p k", p=128), w=[NRM.d], allow_slow_non_contiguous=True)
            DMA(NRM.t[:, 1, l, :], norm_ffn[l].rearrange("(k p) -> p k", p=128), w=[NRM.d], allow_slow_non_contiguous=True)
        DMA(NRM.t[:, 2, 0, :], norm_final.rearrange("(k p) -> p k", p=128), w=[NRM.d], allow_slow_non_contiguous=True)

        with contextlib.ExitStack() as ph:
            CTt = mk(ph, "CTt", [128, 8, 8])
            SC = mk(ph, "SC", [128, 8, 8])
            BM = mk(ph, "BM", [128, 48])
            wst = Rot([mk(ph, "wms%d" % i, [128, 8, 512]) for i in range(2)])
            pm = Rot([mkp(ph, "pm%d" % i, [128, 512]) for i in range(2)])
            DMA(CTt.t[:], cT.rearrange("(k p) c -> p k c", p=128), w=[CTt.d])
            A("activation", dict(out=SC.t[:], in_=CTt.t[:], func=AF.Silu), [CTt.d], [SC.d])
            for l in range(depth):
                DMA(BM.t[:], b_mod[l].rearrange("(t p) -> p t", p=128), w=[BM.d], allow_slow_non_contiguous=True)
                for fb in range(12):
                    ws = wst.next()
                    DMA(ws.t[:], w_mod[l][:, fb * 512:(fb + 1) * 512].rearrange("(k p) f -> p k f", p=128), w=[ws.d])
                    for j in range(4):
                        t = fb * 4 + j
                        p = pm.next()
                        for k in range(8):
                            PE("matmul", dict(out=p.t[:, 0:8], lhsT=ws.t[:, k, j * 128:(j + 1) * 128],
                                                                         rhs=SC.t[:, k, :], start=(k == 0), stop=(k == 7)), [ws.d, SC.d], [p.d])
                        V("tensor_scalar", dict(out=MOD.t[:, l, t, :], in0=p.t[:, 0:8], scalar1=BM.t[:, t:t + 1],
                                                                  scalar2=None, op0=ALU.add), [p.d, BM.d], [MOD.d])
                for k in range(8):
                    V("tensor_scalar", dict(out=GN12.t[:, l, 0, k, :], in0=MOD.t[:, l, 8 + k, :], scalar1=1.0,
                                                          scalar2=NRM.t[:, 0, l, k:k + 1], op0=ALU.add, op1=ALU.mult), [MOD.d, NRM.d], [GN12.d])
                    V("tensor_scalar", dict(out=GN12.t[:, l, 1, k, :], in0=MOD.t[:, l, 32 + k, :], scalar1=1.0,
                                                          scalar2=NRM.t[:, 1, l, k:k + 1], op0=ALU.add, op1=ALU.mult), [MOD.d, NRM.d], [GN12.d])
            S.flush()

        def load_weight(dst, ddeps, src, K, N, stg, engs, chunk=2048):
            for k in range(K):
                for c0 in range(0, N, 2048):
                    w_ = min(2048, N - c0)
                    DMA(dst.t[:, k, c0:c0 + w_], src[k * 128:(k + 1) * 128, c0:c0 + w_], w=[ddeps[k]], q="gpsimd")

        def norm_mod(xt, T, gsel, shift_t0, l, col, sq, pms, rs, tmps, xn):
            A("activation", dict(out=sq.t[:, :, :T], in_=xt.t[:, :, :T], func=AF.Square), [xt.d], [sq.d])
            for k in range(8):
                PE("matmul", dict(out=pms.t[:, :T], lhsT=ONESB.t[:], rhs=sq.t[:, k, :T], start=(k == 0), stop=(k == 7)), [ONESB.d, sq.d], [pms.d])
            A("activation", dict(out=rs.t[:, :T], in_=pms.t[:, :T], func=AF.Sqrt, bias=EPS, scale=1.0), [pms.d], [rs.d])
            V("reciprocal", dict(out=rs.t[:, :T], in_=rs.t[:, :T]), [rs.d], [rs.d])
            for k in range(8):
                tm_ = tmps.next()
                V("scalar_tensor_tensor", dict(out=tm_.t[:, :T], in0=xt.t[:, k, :T],
                                                                 scalar=GN12.t[:, l, gsel, k, col:col + 1], in1=rs.t[:, :T],
                                                                 op0=ALU.mult, op1=ALU.mult), [xt.d, GN12.d, rs.d], [tm_.d])
                A("activation", dict(out=xn.t[:, k, :T], in_=tm_.t[:, :T], func=AF.Identity,
                                                       bias=MOD.t[:, l, shift_t0 + k, col:col + 1], scale=1.0), [tm_.d, MOD.d], [xn.d])

        TOK_TILES = [(0, 256)] + [(256 + i * 512, 512) for i in range(4)]

        for l in range(depth):
            XIN = xT if l == 0 else X1
            d_XIN = [Dep() for _ in range(nseq)] if l == 0 else d_X1

            with contextlib.ExitStack() as ph:
                WIN = mk(ph, "WIN", [128, 8, INDIM], BF16)
                d_win = [Dep() for _ in range(8)]
                stg = None
                xts = Rot([mk(ph, "xt%d" % i, [128, 8, 512]) for i in range(2)])
                sq = mk(ph, "sq", [128, 8, 512], BF16)
                xn = mk(ph, "xn", [128, 8, 512], BF16)
                rs = mk(ph, "rs", [128, 512])
                tmps = Rot([mk(ph, "tmp%d" % i, [128, 512]) for i in range(2)])
                gst = Rot([mk(ph, "gst%d" % i, [128, 6, 512], BF16) for i in range(2)])
                zst = Rot([mk(ph, "zst%d" % i, [32, 512]) for i in range(2)])
                tmst = Rot([mk(ph, "tmst%d" % i, [128, NTM]) for i in range(2)])
                pms = mkp(ph, "pms", [128, 512])
                pf = Rot([mkp(ph, "pf%d" % i, [128, 512]) for i in range(3)])
                pt = Rot([mkp(ph, "pt%d" % i, [128, 512]) for i in range(3)])
                load_weight(WIN, d_win, w_in[l], 8, INDIM, stg, ["gpsimd", "vector", "scalar"], 512)
                for s in range(nseq):
                    for (t0, T) in TOK_TILES:
                        col = 4 if t0 < CTXL else s
                        xt = xts.next()
                        DMA(xt.t[:, :, :T], XIN[s][:, t0:t0 + T].rearrange("(k p) t -> p k t", p=128), r=[d_XIN[s]], w=[xt.d])
                        norm_mod(xt, T, 0, 0, l, col, sq, pms, rs, tmps, xn)
                        gs = None
                        for mt in range(25):
                            if mt % 6 == 0 and mt < 24:
                                gs = gst.next()
                            p = pf.next()
                            c0 = 2592 + mt * 128 if mt < 24 else 2560
                            M = 128 if mt < 24 else 32
                            for k in range(8):
                                PE("matmul", dict(out=p.t[:M, :T], lhsT=WIN.t[:, k, c0:c0 + M], rhs=xn.t[:, k, :T],
                                                                             start=(k == 0), stop=(k == 7)), [d_win[k], xn.d], [p.d])
                            if mt < 24:
                                A("activation", dict(out=gs.t[:, mt % 6, :T], in_=p.t[:, :T], func=AF.Sigmoid), [p.d], [gs.d])
                                if mt % 6 == 5:
                                    m0 = (mt // 6) * 6
                                    DMA(GATE[s][m0 * 128:(m0 + 6) * 128, t0:t0 + T].rearrange("(m p) t -> p m t", p=128), gs.t[:, :, :T], r=[gs.d], w=[d_GATE[s]], q="scalar")
                            else:
                                zs = zst.next()
                                V("tensor_copy", dict(out=zs.t[:, :T], in_=p.t[:32, :T]), [p.d], [zs.d])
                                DMA(ZF[s][:, t0:t0 + T], zs.t[:, :T], r=[zs.d], w=[d_ZF[s]], q="gpsimd")
                        for sub in range(T // 128):
                            tok0 = t0 + sub * 128
                            ts_ = tmst.next()
                            groups = [(0, 256, "copy"), (256, 640, "rope"), (640, 1024, "copy"), (1024, 1408, "silu"),
                                      (1408, 1792, "copy"), (1792, 2176, "copy"), (2176, 2560, "silu")]
                            for gi, (c0, c1, kind) in enumerate(groups):
                                p = pt.next()
                                n = c1 - c0
                                for k in range(8):
                                    PE("matmul", dict(out=p.t[:, :n], lhsT=xn.t[:, k, sub * 128:(sub + 1) * 128], rhs=WIN.t[:, k, c0:c1],
                                        start=(k == 0), stop=(k == 7)), [d_win[k], xn.d], [p.d])
                                if kind == "silu":
                                    A("activation", dict(out=ts_.t[:, c0:c1], in_=p.t[:, :n], func=AF.Silu), [p.d], [ts_.d])
                                elif kind == "rope" and tok0 >= CTXL:
                                    ch = (tok0 - CTXL) // 128
                                    cosb = bc(CST.t[:, C_COS + ch * 24:C_COS + ch * 24 + 24], [128, 8, 24], 1)
                                    sinb = bc(CST.t[:, C_SIN + ch * 24:C_SIN + ch * 24 + 24], [128, 8, 24], 1)
                                    pv = p.t[:, :384].rearrange("p (h m two) -> p h m two", h=8, two=2)
                                    ov = ts_.t[:, c0:c1].rearrange("p (h m two) -> p h m two", h=8, two=2)
                                    ra = tmps.next()
                                    rb = tmps.next()
                                    rav = ra.t[:, 0:192].rearrange("p (h m) -> p h m", h=8)
                                    rbv = rb.t[:, 0:192].rearrange("p (h m) -> p h m", h=8)
                                    V("tensor_tensor", dict(out=rav, in0=pv[:, :, :, 0], in1=cosb, op=ALU.mult), [p.d, CST.d], [ra.d])
                                    V("tensor_tensor", dict(out=rbv, in0=pv[:, :, :, 1], in1=sinb, op=ALU.mult), [p.d, CST.d], [rb.d])
                                    V("tensor_tensor", dict(out=ov[:, :, :, 0], in0=rav, in1=rbv, op=ALU.subtract), [ra.d, rb.d], [ts_.d])
                                    V("tensor_tensor", dict(out=rav, in0=pv[:, :, :, 0], in1=sinb, op=ALU.mult), [p.d, CST.d], [ra.d])
                                    V("tensor_tensor", dict(out=rbv, in0=pv[:, :, :, 1], in1=cosb, op=ALU.mult), [p.d, CST.d], [rb.d])
                                    V("tensor_tensor", dict(out=ov[:, :, :, 1], in0=rav, in1=rbv, op=ALU.add), [ra.d, rb.d], [ts_.d])
                                else:
                                    V("tensor_copy", dict(out=ts_.t[:, c0:c1], in_=p.t[:, :n]), [p.d], [ts_.d])
                            DMA(TM[s][tok0:tok0 + 128, :], ts_.t[:], r=[ts_.d], w=[d_TM[s]], q="gpsimd")
                S.flush()
            if upto in ("ph1", "ph1s"):
                break

            with contextlib.ExitStack() as mx:
                G0 = mk(mx, "G0", [128, 32, 128], BF16)
                WBT = mk(mx, "WBT", [128, 32, 2, 64], BF16)
                WC = mk(mx, "WC", [64, 32, 2, 128], BF16)
                ARI = mk(mx, "ARI", [64, 2, 2, 32])
                REQ = mk(mx, "REQ", [128, 2, 3, 192])
                RDEC = mk(mx, "RDEC", [48, 8])
                GNR = mk(mx, "GNR", [128, 384])
                GNG = mk(mx, "GNG", [128, 384])
                GW = mk(mx, "GW", [32, 2, 192])
                DSK = mk(mx, "DSK", [128, 256])
                GLUB = mk(mx, "GLUB", [128, 2])
                GLUW = mk(mx, "GLUW", [128, 2, 256], BF16)
                pbank = [mkp(mx, "pb%d" % i, [128, 512]) for i in range(6)]
                pOs = Rot([pbank[1], pbank[5]])
                pSs = Rot([pbank[2], pbank[4]])
                pT = mkp(mx, "pT", [128, 8, 128], BF16)
                pT2 = mkp(mx, "pT2", [128, 8, 128], BF16)
                pA, pO, pS, pM, pR, pY = pbank[0], pbank[1], pbank[2], pbank[3], pbank[4], pbank[5]
                MAGIC = 12582912.0
                TWO_PI = 2.0 * math.pi
                C1 = 6.28125
                C2 = TWO_PI - C1
                PI_S = 3.1415925

                with contextlib.ExitStack() as pp:
                    LAM = mk(pp, "LAM", [64, 2, 32])
                    DTL = mk(pp, "DTL", [64, 32])
                    LD = mk(pp, "LD", [64, 2, 32])
                    POW = mk(pp, "POW", [64, 2, 32, 40])
                    WK = [mk(pp, "wk%d" % i, [64, 32, 40]) for i in range(4)]
                    SCT = mk(pp, "SCT", [64, 2, 32, 40])
                    FF = mk(pp, "FF", [64, 2, 32])
                    SM = [mk(pp, "sm%d" % i, [64, 32]) for i in range(4)]
                    BRI = mk(pp, "BRI", [64, 2, 16, 16])
                    CRI = mk(pp, "CRI", [64, 2, 16, 16])
                    BB = mk(pp, "BB", [64, 2, 32, 16])
                    BFt = mk(pp, "BFt", [64, 2, 8, 128])
                    CFt = mk(pp, "CFt", [64, 2, 8, 128])
                    WBm = mk(pp, "WBm", [64, 2, 8, 128])
                    WCm = mk(pp, "WCm", [64, 2, 8, 128])
                    t1 = mk(pp, "t1", [64, 1024])
                    t2 = mk(pp, "t2", [64, 1024])
                    wst2 = Rot([mk(pp, "wst2_%d" % i, [128, 256]) for i in range(2)])
                    LGB = mk(pp, "LGB", [128, 8])
                    NLGB = mk(pp, "NLGB", [128, 8])
                    P1 = mk(pp, "P1", [128, 2, 2, 48])

                    DMA(LAM.t[:], s5lam[l], w=[LAM.d])
                    DMA(DTL.t[:], s5dt[l].partition_broadcast(64), w=[DTL.d])
                    DMA(BRI.t[:], s5B[l], w=[BRI.d])
                    DMA(CRI.t[:], s5C[l], w=[CRI.d])
                    DMA(DSK.t[:], s5_d[l].partition_broadcast(128), w=[DSK.d])
                    DMA(GLUB.t[:], s5_glu_b[l].rearrange("(c p) -> p c", p=128), w=[GLUB.d], allow_slow_non_contiguous=True)
                    DMA(GNR.t[:], ret_gn[l].partition_broadcast(128), w=[GNR.d])
                    DMA(GNG.t[:], gla_norm[l].partition_broadcast(128), w=[GNG.d])
                    DMA(LGB.t[:], ret_ld[l].partition_broadcast(128), w=[LGB.d])
                    G("memset", dict(ap=GW.t[:], constant=0.0), w=[GW.d])
                    DMA(GW.t[0:16, :, :], gla_gw[l].rearrange("d r c -> r d c"), w=[GW.d])
                    DMA(GW.t[16:17, :, :], gla_gb[l].rearrange("(o d) c -> o d c", o=1), w=[GW.d])
                    for kt in range(2):
                        st = wst2.next()
                        DMA(st.t[:], s5_glu_w[l][kt * 128:(kt + 1) * 128, :], w=[st.d])
                        V("tensor_copy", dict(out=GLUW.t[:, kt, :], in_=st.t[:]), [st.d], [GLUW.d])

                    iota48 = CST.t[:, C_IOTA:C_IOTA + 1].broadcast_to([128, 48])
                    V("tensor_scalar", dict(out=P1.t[:, 0, 0, :], in0=iota48, scalar1=1.0, scalar2=None, op0=ALU.add), [CST.d], [P1.d])
                    V("tensor_scalar", dict(out=P1.t[:, 1, 0, :], in0=iota48, scalar1=-1.0, scalar2=128.0, op0=ALU.mult, op1=ALU.add), [CST.d], [P1.d])
                    V("tensor_scalar", dict(out=P1.t[:, 0, 1, :], in0=iota48, scalar1=-1.0, scalar2=127.0, op0=ALU.mult, op1=ALU.add), [CST.d], [P1.d])
                    V("tensor_copy", dict(out=P1.t[:, 1, 1, :], in_=iota48), [CST.d], [P1.d])
                    V("tensor_scalar", dict(out=NLGB.t[:], in0=LGB.t[:], scalar1=-1.0, scalar2=None, op0=ALU.mult), [LGB.d], [NLGB.d])
                    lnk = math.log(48.0 ** -0.5)
                    LNK = mk(pp, "LNK", [128, 1])
                    G("memset", dict(ap=LNK.t[:], constant=lnk), w=[LNK.d])
                    for dd in range(2):
                        for h in range(4):
                            ix = dd * 4 + h
                            hs = slice(h * 48, (h + 1) * 48)
                            A("activation", dict(out=REQ.t[:, dd, 0, hs], in_=P1.t[:, dd, 0, :], func=AF.Exp,
                                                                         scale=LGB.t[:, ix:ix + 1]), [P1.d, LGB.d], [REQ.d])
                            A("activation", dict(out=REQ.t[:, dd, 1, hs], in_=P1.t[:, dd, 0, :], func=AF.Exp,
                                                                         scale=NLGB.t[:, ix:ix + 1], bias=LNK.t[:, 0:1]), [P1.d, NLGB.d, LNK.d], [REQ.d])
                            A("activation", dict(out=REQ.t[:, dd, 2, hs], in_=P1.t[:, dd, 1, :], func=AF.Exp,
                                                                         scale=LGB.t[:, ix:ix + 1], bias=LNK.t[:, 0:1]), [P1.d, LGB.d, LNK.d], [REQ.d])
                    A("activation", dict(out=RDEC.t[:], in_=LGB.t[0:48, :], func=AF.Exp, scale=128.0), [LGB.d], [RDEC.d])

                    A("activation", dict(out=DTL.t[:], in_=DTL.t[:], func=AF.Exp), [DTL.d], [DTL.d])
                    for c in range(2):
                        V("tensor_tensor", dict(out=LD.t[:, c, :], in0=LAM.t[:, c, :], in1=DTL.t[:], op=ALU.mult), [LAM.d, DTL.d], [LD.d])
                    NVv = CST.t[0:64, C_NV:C_NV + 1280].rearrange("p (g t) -> p g t", g=32)
                    V("tensor_tensor", dict(out=WK[0].t[:], in0=NVv, in1=bc(LD.t[:, 0, :], [64, 32, 40], 2), op=ALU.mult), [CST.d, LD.d], [WK[0].d])
                    A("activation", dict(out=WK[0].t[:], in_=WK[0].t[:], func=AF.Exp), [WK[0].d], [WK[0].d])
                    V("tensor_tensor", dict(out=WK[1].t[:], in0=NVv, in1=bc(LD.t[:, 1, :], [64, 32, 40], 2), op=ALU.mult), [CST.d, LD.d], [WK[1].d])
                    for c in range(2):
                        if c == 0:
                            V("tensor_scalar", dict(out=WK[2].t[:], in0=WK[1].t[:], scalar1=math.pi / 2, scalar2=None, op0=ALU.add), [WK[1].d], [WK[2].d])
                            src_ = WK[2]
                        else:
                            src_ = WK[1]
                        V("tensor_scalar", dict(out=WK[3].t[:], in0=src_.t[:], scalar1=1.0 / TWO_PI, scalar2=MAGIC, op0=ALU.mult, op1=ALU.add), [src_.d], [WK[3].d])
                        V("tensor_scalar", dict(out=WK[3].t[:], in0=WK[3].t[:], scalar1=MAGIC, scalar2=None, op0=ALU.subtract), [WK[3].d], [WK[3].d])
                        V("scalar_tensor_tensor", dict(out=src_.t[:], in0=WK[3].t[:], scalar=-C1, in1=src_.t[:], op0=ALU.mult, op1=ALU.add), [WK[3].d, src_.d], [src_.d])
                        V("scalar_tensor_tensor", dict(out=src_.t[:], in0=WK[3].t[:], scalar=-C2, in1=src_.t[:], op0=ALU.mult, op1=ALU.add), [WK[3].d, src_.d], [src_.d])
                        V("tensor_scalar", dict(out=src_.t[:], in0=src_.t[:], scalar1=-PI_S, scalar2=PI_S, op0=ALU.max, op1=ALU.min), [src_.d], [src_.d])
                        A("activation", dict(out=SCT.t[:, c, :, :], in_=src_.t[:], func=AF.Sin), [src_.d], [SCT.d])
                    for c in range(2):
                        V("tensor_tensor", dict(out=POW.t[:, c, :, :], in0=WK[0].t[:], in1=SCT.t[:, c, :, :], op=ALU.mult), [WK[0].d, SCT.d], [POW.d])
                    a_re = POW.t[:, 0, :, 32]
                    a_im = POW.t[:, 1, :, 32]
                    lre = LAM.t[:, 0, :]
                    lim = LAM.t[:, 1, :]
                    s0, s1_, s2_, s3_ = SM
                    pd = [POW.d, LAM.d]
                    V("tensor_scalar", dict(out=s0.t[:], in0=a_re, scalar1=-1.0, scalar2=None, op0=ALU.add), pd, [s0.d])
                    V("tensor_tensor", dict(out=s1_.t[:], in0=lre, in1=lre, op=ALU.mult), pd, [s1_.d])
                    V("tensor_tensor", dict(out=s2_.t[:], in0=lim, in1=lim, op=ALU.mult), pd, [s2_.d])
                    V("tensor_tensor", dict(out=s1_.t[:], in0=s1_.t[:], in1=s2_.t[:], op=ALU.add), [s1_.d, s2_.d], [s1_.d])
                    V("reciprocal", dict(out=s1_.t[:], in_=s1_.t[:]), [s1_.d], [s1_.d])
                    V("tensor_tensor", dict(out=s2_.t[:], in0=s0.t[:], in1=lre, op=ALU.mult), pd + [s0.d], [s2_.d])
                    V("tensor_tensor", dict(out=s3_.t[:], in0=a_im, in1=lim, op=ALU.mult), pd, [s3_.d])
                    V("tensor_tensor", dict(out=s2_.t[:], in0=s2_.t[:], in1=s3_.t[:], op=ALU.add), [s2_.d, s3_.d], [s2_.d])
                    V("tensor_tensor", dict(out=FF.t[:, 0, :], in0=s2_.t[:], in1=s1_.t[:], op=ALU.mult), [s2_.d, s1_.d], [FF.d])
                    V("tensor_tensor", dict(out=s2_.t[:], in0=a_im, in1=lre, op=ALU.mult), pd, [s2_.d])
                    V("tensor_tensor", dict(out=s3_.t[:], in0=s0.t[:], in1=lim, op=ALU.mult), pd + [s0.d], [s3_.d])
                    V("tensor_tensor", dict(out=s2_.t[:], in0=s2_.t[:], in1=s3_.t[:], op=ALU.subtract), [s2_.d, s3_.d], [s2_.d])
                    V("tensor_tensor", dict(out=FF.t[:, 1, :], in0=s2_.t[:], in1=s1_.t[:], op=ALU.mult), [s2_.d, s1_.d], [FF.d])

                    def cmul(outr, outi, ar, ai, br, bi, shape, rd, wd, neg=False):
                        n = 1
                        for x in shape[1:]:
                            n *= x
                        pat = {2: "p (a) -> p a", 3: "p (a b) -> p a b", 4: "p (a b c) -> p a b c"}[len(shape)]
                        kw = {}
                        for nm, sz in zip("abc", shape[1:]):
                            kw[nm] = sz
                        v1 = t1.t[:shape[0], :n].rearrange(pat, **kw)
                        v2 = t2.t[:shape[0], :n].rearrange(pat, **kw)
                        V("tensor_tensor", dict(out=v1, in0=ar, in1=br, op=ALU.mult), rd, [t1.d])
                        V("tensor_tensor", dict(out=v2, in0=ai, in1=bi, op=ALU.mult), rd, [t2.d])
                        V("tensor_tensor", dict(out=outr, in0=v1, in1=v2, op=ALU.subtract), [t1.d, t2.d], wd)
                        V("tensor_tensor", dict(out=v1, in0=ar, in1=bi, op=ALU.mult), rd, [t1.d])
                        V("tensor_tensor", dict(out=v2, in0=ai, in1=br, op=ALU.mult), rd, [t2.d])
                        if neg:
                            V("tensor_tensor", dict(out=v1, in0=v1, in1=v2, op=ALU.add), [t1.d, t2.d], [t1.d])
                            V("tensor_scalar", dict(out=outi, in0=v1, scalar1=-1.0, scalar2=None, op0=ALU.mult), [t1.d], wd)
                        else:
                            V("tensor_tensor", dict(out=outi, in0=v1, in1=v2, op=ALU.add), [t1.d, t2.d], wd)

                    for dd in range(2):
                        gs = slice(dd * 16, (dd + 1) * 16)
                        fr = bc(FF.t[:, 0, gs], [64, 16, 16], 2)
                        fi = bc(FF.t[:, 1, gs], [64, 16, 16], 2)
                        cmul(BB.t[:, 0, gs, :], BB.t[:, 1, gs, :], fr, fi, BRI.t[:, 0, :, :], BRI.t[:, 1, :, :], [64, 16, 16],
                             [FF.d, BRI.d], [BB.d])
                    a8r = POW.t[:, 0, :, 33]
                    a8i = POW.t[:, 1, :, 33]
                    V("tensor_copy", dict(out=ARI.t[:, 0, 0, :], in_=a8r), [POW.d], [ARI.d])
                    V("tensor_copy", dict(out=ARI.t[:, 0, 1, :], in_=a8r), [POW.d], [ARI.d])
                    V("tensor_scalar", dict(out=ARI.t[:, 1, 0, :], in0=a8i, scalar1=-1.0, scalar2=None, op0=ALU.mult), [POW.d], [ARI.d])
                    V("tensor_copy", dict(out=ARI.t[:, 1, 1, :], in_=a8i), [POW.d], [ARI.d])

                    for gb in range(4):
                        dd = gb // 2
                        g0 = (gb % 2) * 8
                        gsl = slice(gb * 8, gb * 8 + 8)
                        sh = [64, 8, 8, 16]
                        v4 = lambda tl, c: tl.t[:, c, :, :].rearrange("p g (j c) -> p g j c", j=8)

                        def pw(c, tsel):
                            return bc(POW.t[:, c, gsl, tsel * 8:tsel * 8 + 8], sh, 3)

                        bbr = bc(BB.t[:, 0, gsl, :], sh, 2)
                        bbi = bc(BB.t[:, 1, gsl, :], sh, 2)
                        cr = bc(CRI.t[:, 0, g0:g0 + 8, :], sh, 2)
                        ci = bc(CRI.t[:, 1, g0:g0 + 8, :], sh, 2)
                        cmul(v4(BFt, 0), v4(BFt, 1), pw(0, 0), pw(1, 0), bbr, bbi, sh, [POW.d, BB.d], [BFt.d])
                        cmul(v4(CFt, 0), v4(CFt, 1), pw(0, 1), pw(1, 1), cr, ci, sh, [POW.d, CRI.d], [CFt.d], neg=True)
                        cmul(v4(WBm, 0), v4(WBm, 1), pw(0, 2), pw(1, 2), bbr, bbi, sh, [POW.d, BB.d], [WBm.d])
                        cmul(v4(WCm, 0), v4(WCm, 1), pw(0, 3), pw(1, 3), cr, ci, sh, [POW.d, CRI.d], [WCm.d], neg=True)
                        for c in range(2):
                            V("tensor_copy", dict(out=WC.t[:, gsl, c, :], in_=WCm.t[:, c, :, :]), [WCm.d], [WC.d])
                        mcol = C_MGF if dd == 0 else C_MGB
                        for gl in range(8):
                            gd = gb * 8 + gl
                            pg = pY if gl % 2 == 0 else pM
                            PE("matmul", dict(out=pg.t[:, 0:128], lhsT=BFt.t[:, 0, gl, :], rhs=CFt.t[:, 0, gl, :], start=True, stop=False), [BFt.d, CFt.d], [pg.d])
                            PE("matmul", dict(out=pg.t[:, 0:128], lhsT=BFt.t[:, 1, gl, :], rhs=CFt.t[:, 1, gl, :], start=False, stop=True), [BFt.d, CFt.d], [pg.d])
                            V("tensor_tensor", dict(out=G0.t[:, gd, :], in0=pg.t[:, 0:128], in1=CST.t[:, mcol:mcol + 128], op=ALU.mult), [pg.d, CST.d], [G0.d])
                            for c in range(2):
                                pr = pR if c == 0 else pS
                                PE("transpose", dict(out=pr.t[:, 0:64], in_=WBm.t[:, c, gl, :], identity=CST.t[0:64, C_ID:C_ID + 64]), [WBm.d, CST.d], [pr.d])
                                A("activation", dict(out=WBT.t[:, gd, c, :], in_=pr.t[:, 0:64], func=AF.Copy), [pr.d], [WBT.d])
                    S.flush()

                U32 = mk(mx, "U32", [128, 8, 256])
                UB = mk(mx, "UB", [128, 16, 128], BF16)
                U8 = mk(mx, "U8", [128, 16, 288], BF16)
                VV = mk(mx, "VV", [64, 289, 2, 32], BF16)
                HS = Rot([mk(mx, "HS%d" % i, [64, 2, 32]) for i in range(2)])
                T1s = mk(mx, "T1s", [64, 2, 32])
                T2s = mk(mx, "T2s", [64, 2, 32])
                YTt = mk(mx, "YTt", [128, 8, 256])
                YYb = mk(mx, "YYb", [128, 8, 256], BF16)
                YYT = mk(mx, "YYT", [128, 2, 8, 128], BF16)
                SGL = mk(mx, "SGL", [128, 4, 128])
                S5O = mk(mx, "S5O", [128, 2, 1024], BF16)
                QKs = Rot([mk(mx, "QK%d" % i, [128, 1152]) for i in range(4)])
                ZAs = Rot([mk(mx, "ZA%d" % i, [32, 128]) for i in range(2)])
                E1 = mk(mx, "E1", [128, 192])
                NL = mk(mx, "NL", [128, 192])
                EQts = Rot([mk(mx, "EQt%d" % i, [128, 3, 192]) for i in range(2)])
                DECts = Rot([mk(mx, "DECt%d" % i, [48, 8]) for i in range(2)])
                QTs = Rot([mk(mx, "QT%d" % i, [128, 192], BF16) for i in range(2)])
                KTs = Rot([mk(mx, "KT%d" % i, [128, 192], BF16) for i in range(2)])
                KHs = Rot([mk(mx, "KH%d" % i, [128, 192], BF16) for i in range(2)])
                VBs = Rot([mk(mx, "VB%d" % i, [128, 384], BF16) for i in range(2)])
                QKTs = Rot([mk(mx, "QKT%d" % i, [48, 8, 128], BF16) for i in range(2)])
                PTs = Rot([mk(mx, "PT%d" % i, [128, 4, 128], BF16) for i in range(2)])
                S32 = [mk(mx, "S32_%d" % i, [48, 4, 96]) for i in range(2)]
                SBs = Rot([mk(mx, "SBf_%d" % i, [48, 4, 96], BF16) for i in range(2)])
                OST = Rot([mk(mx, "OST%d" % i, [128, 384]) for i in range(2)])
                OBt = Rot([mk(mx, "OBt%d" % i, [128, 384]) for i in range(4)])
                OT = mk(mx, "OT", [128, 4, 96])
                SQt = mk(mx, "SQt", [128, 4, 96])
                YN = mk(mx, "YN", [128, 4, 96])
                YB = mk(mx, "YB", [128, 384], BF16)
                ST = [mk(mx, "st%d" % i, [128, 4]) for i in range(5)]
                YFM = [Rot([mk(mx, "YFM%d_%d" % (m_, i), [128, 3, 512], BF16) for i in range(1)]) for m_ in range(2)]

                for za in ZAs.items:
                    G("memset", dict(ap=za.t[:], constant=1.0), w=[za.d])
                G("memset", dict(ap=VV.t[:], constant=0.0), w=[VV.d])

                NT_S5 = [(0, 32), (32, 128), (160, 128)]

                def s5_pre(s):
                    TMv = TM[s].rearrange("(n j) c -> n j c", j=8)
                    for (n0, nn) in NT_S5:
                        DMA(U32.t[:nn], TMv[n0:n0 + nn, :, 0:256], r=[d_TM[s]], w=[U32.d])
                        V("tensor_copy", dict(out=UB.t[:nn].rearrange("p g (j c) -> p j g c", j=8), in_=U32.t[:nn].rearrange("p j (g c) -> p j g c", g=16)), [U32.d], [UB.d])
                        for gh in range(2):
                            pt_ = pT if gh == 0 else pT2
                            for gl in range(8):
                                g = gh * 8 + gl
                                PE("transpose", dict(out=pt_.t[:, gl, :nn], in_=UB.t[:nn, g, :],
                                                                                      identity=IDB.t[:nn, :nn]), [UB.d, IDB.d], [pt_.d])
                            V("tensor_copy", dict(out=U8.t[:, gh * 8:(gh + 1) * 8, n0:n0 + nn], in_=pt_.t[:, :, :nn]), [pt_.d], [U8.d])
                    ev = 0
                    for gd in range(32):
                        g = gd % 16
                        for c in range(2):
                            px = pM if (gd * 2 + c) % 2 == 0 else pR
                            PE("matmul", dict(out=px.t[:64, 0:288], lhsT=WBT.t[:, gd, c, :], rhs=U8.t[:, g, :], start=True, stop=True), [WBT.d, U8.d], [px.d])
                            if gd < 16:
                                outs = [(VV.t[:, 1:289, c, gd], px.t[:64, 0:288])]
                            else:
                                outs = [(VV.t[:, 256:288, c, gd], px.t[:64, 0:32]), (VV.t[:, 0:256, c, gd], px.t[:64, 32:288])]
                            for (o_, i_) in outs:
                                if ev % 2 == 0:
                                    V("tensor_copy", dict(out=o_, in_=i_), [px.d], [VV.d])
                                else:
                                    A("activation", dict(out=o_, in_=i_, func=AF.Copy), [px.d], [VV.d])
                                ev += 1
                    h = HS.next()
                    G("memset", dict(ap=h.t[:], constant=0.0), w=[h.d])
                    PS_VV = VV.t[:, 0, 0, 0:1].ap[0][0]
                    vv_off0 = VV.t[:, 0, 0, 0:1].offset
                    for k in range(288):
                        hn = HS.next()
                        sf = k + 1
                        sb_ = 287 - k
                        hsw = bass.AP(tensor=h.t[:].tensor, offset=h.t[:, 1, :].offset, ap=[list(h.t[:].ap[0]), [-32, 2], [1, 32]])
                        xap = bass.AP(tensor=VV.t[:].tensor, offset=vv_off0 + sf * 64, ap=[[PS_VV, 64], [32, 2], [(sb_ - sf) * 64 + 16, 2], [1, 16]])
                        G("tensor_tensor", dict(out=T1s.t[:], in0=h.t[:], in1=ARI.t[:, 0, :, :], op=ALU.mult), [h.d, ARI.d], [T1s.d])
                        G("tensor_tensor", dict(out=T2s.t[:], in0=hsw, in1=ARI.t[:, 1, :, :], op=ALU.mult), [h.d, ARI.d], [T2s.d])
                        G("tensor_tensor", dict(out=T1s.t[:], in0=T1s.t[:], in1=T2s.t[:], op=ALU.add), [T1s.d, T2s.d], [T1s.d])
                        G("tensor_tensor", dict(out=hn.t[:].rearrange("p c (d g) -> p c d g", d=2), in0=T1s.t[:].rearrange("p c (d g) -> p c d g", d=2), in1=xap, op=ALU.add),
                          [T1s.d, VV.d], [hn.d])
                        G("tensor_copy", dict(out=xap, in_=hn.t[:].rearrange("p c (d g) -> p c d g", d=2)), [hn.d], [VV.d])
                        h = hn

                def s5_post(s):
                    TMv = TM[s].rearrange("(n j) c -> n j c", j=8)
                    YSv = YS5[s].rearrange("(c p) t -> p c t", p=128)
                    for (n0, nn) in NT_S5:
                        bs0 = 257 + n0 if n0 == 0 else n0 - 31
                        DMA(U32.t[:nn], TMv[n0:n0 + nn, :, 0:256], r=[d_TM[s]], w=[U32.d])
                        for g in range(16):
                            mm = [(U8.t[:, g, n0:n0 + nn], G0.t[:, g, :], [U8.d, G0.d]),
                                  (U8.t[:, g, n0:n0 + nn], G0.t[:, 16 + g, :], [U8.d, G0.d]),
                                  (VV.t[:, n0:n0 + nn, 0, g], WC.t[:, g, 0, :], [VV.d, WC.d]),
                                  (VV.t[:, n0:n0 + nn, 1, g], WC.t[:, g, 1, :], [VV.d, WC.d]),
                                  (VV.t[:, bs0:bs0 + nn, 0, 16 + g], WC.t[:, 16 + g, 0, :], [VV.d, WC.d]),
                                  (VV.t[:, bs0:bs0 + nn, 1, 16 + g], WC.t[:, 16 + g, 1, :], [VV.d, WC.d])]
                            for mi, (lh, rh, rd) in enumerate(mm):
                                PE("matmul", dict(out=pY.t[:nn, 0:128], lhsT=lh, rhs=rh, start=(mi == 0), stop=(mi == 5)), rd, [pY.d])
                            o_ = YTt.t[:nn, :, g * 16:(g + 1) * 16]
                            i_ = pY.t[:nn, 0:128].rearrange("p (i c) -> p i c", i=8)
                            if g % 2 == 0:
                                V("tensor_copy", dict(out=o_, in_=i_), [pY.d], [YTt.d])
                            else:
                                A("activation", dict(out=o_, in_=i_, func=AF.Copy), [pY.d], [YTt.d])
                        dsk = bc(DSK.t[:nn, :], [nn, 8, 256], 1)
                        V("tensor_tensor", dict(out=U32.t[:nn], in0=U32.t[:nn], in1=dsk, op=ALU.mult), [U32.d, DSK.d], [U32.d])
                        V("tensor_tensor", dict(out=YTt.t[:nn], in0=YTt.t[:nn], in1=U32.t[:nn], op=ALU.add), [YTt.d, U32.d], [YTt.d])
                        V("tensor_tensor", dict(out=U32.t[:nn], in0=YTt.t[:nn], in1=YTt.t[:nn], op=ALU.mult), [YTt.d], [U32.d])
                        V("tensor_scalar", dict(out=U32.t[:nn], in0=U32.t[:nn], scalar1=0.044715, scalar2=1.0, op0=ALU.mult, op1=ALU.add), [U32.d], [U32.d])
                        V("tensor_tensor", dict(out=U32.t[:nn], in0=U32.t[:nn], in1=YTt.t[:nn], op=ALU.mult), [U32.d, YTt.d], [U32.d])
                        A("activation", dict(out=U32.t[:nn], in_=U32.t[:nn], func=AF.Sigmoid, scale=1.5957691216), [U32.d], [U32.d])
                        V("tensor_tensor", dict(out=YYb.t[:nn], in0=YTt.t[:nn], in1=U32.t[:nn], op=ALU.mult), [YTt.d, U32.d], [YYb.d])
                        for ih in range(2):
                            pt_ = pT if ih == 0 else pT2
                            for il in range(4):
                                i = ih * 4 + il
                                for ct in range(2):
                                    PE("transpose", dict(out=pt_.t[:, il * 2 + ct, :nn], in_=YYb.t[:nn, i, ct * 128:(ct + 1) * 128],
                                                                                                 identity=IDB.t[:nn, :nn]), [YYb.d, IDB.d], [pt_.d])
                            for ct in range(2):
                                i_ = pt_.t[:, :, :nn].rearrange("p (i c) n -> p i c n", c=2)[:, :, ct, :]
                                V("tensor_copy", dict(out=YYT.t[:, ct, ih * 4:(ih + 1) * 4, :nn], in_=i_), [pt_.d], [YYT.d])
                        ntok = nn * 8
                        for co in range(2):
                            for ih in range(2):
                                for kt in range(2):
                                    PE("matmul", dict(out=pO.t[:, 0:4 * nn].rearrange("p (i n) -> p i n", i=4),
                                                                                       lhsT=GLUW.t[:, kt, co * 128:(co + 1) * 128],
                                                                                       rhs=YYT.t[:, kt, ih * 4:(ih + 1) * 4, :nn], start=(kt == 0), stop=(kt == 1)), [GLUW.d, YYT.d], [pO.d])
                                A("activation", dict(out=SGL.t[:, :, :nn], in_=pO.t[:, 0:4 * nn].rearrange("p (i n) -> p i n", i=4),
                                                                      func=AF.Sigmoid, bias=GLUB.t[:, co:co + 1], scale=1.0), [pO.d, GLUB.d], [SGL.d])
                                o_ = S5O.t[:, co, 0:ntok].rearrange("p (n i) -> p i n", i=8)[:, ih * 4:(ih + 1) * 4, :]
                                V("tensor_tensor", dict(out=o_, in0=YYT.t[:, co, ih * 4:(ih + 1) * 4, :nn], in1=SGL.t[:, :, :nn],
                                                                                       op=ALU.mult), [YYT.d, SGL.d], [S5O.d])
                        DMA(YSv[:, :, n0 * 8:n0 * 8 + ntok], S5O.t[:, :, 0:ntok], r=[S5O.d], w=[d_YS5[s]], q="gpsimd")

                def attn(s, kind):
                    gla_ = kind == "gla"
                    Cc = 128
                    NCH = LT // Cc
                    nctx = CTXL // Cc
                    c0 = 1408 if gla_ else 256
                    OBS, d_OBS = (OBG, d_OBG) if gla_ else (OBR, d_OBR)
                    YD, d_YD = (YGLA, d_YGLA) if gla_ else (YRET, d_YRET)
                    GNT = GNG if gla_ else GNR
                    S32t = S32[1 if gla_ else 0]
                    yrot = YFM[1 if gla_ else 0]
                    YDv = YD[s].rearrange("(c p) t -> p c t", p=128)
                    yfh = [None]
                    sbh = [None]

                    def front(c, dd):
                        cx = {"c": c, "dd": dd}
                        tok0 = c * Cc
                        mcol_incl = C_MF if dd == 0 else C_MB
                        mcol_rest = C_MBS if dd == 0 else C_MFS
                        qk = QKs.next()
                        cx["qk"] = qk
                        DMA(qk.t[:Cc, :], TM[s][tok0:tok0 + Cc, c0:c0 + 1152], r=[d_TM[s]], w=[qk.d])
                        if dd == 0:
                            ob = OBt.next()
                            cx["ob"] = ob
                            DMA(ob.t[:Cc, :], OBS[s][tok0:tok0 + Cc, :], r=[d_OBS[s]], w=[ob.d])
                        if gla_:
                            za = ZAs.next()
                            eqt = EQts.next()
                            dect = DECts.next()
                            DMA(za.t[0:16, :Cc], ZF[s][dd * 16:(dd + 1) * 16, tok0:tok0 + Cc], r=[d_ZF[s]], w=[za.d])
                            PE("matmul", dict(out=pM.t[:Cc, 0:192], lhsT=za.t[0:17, :Cc], rhs=GW.t[0:17, dd, :], start=True, stop=True), [za.d, GW.d], [pM.d])
                            A("activation", dict(out=E1.t[:], in_=pM.t[:Cc, 0:192], func=AF.Exp, scale=-1.0), [pM.d], [E1.d])
                            A("activation", dict(out=NL.t[:], in_=E1.t[:], func=AF.Ln, bias=1.0, scale=1.0), [E1.d], [NL.d])
                            PE("matmul", dict(out=pM.t[:Cc, 192:384], lhsT=CST.t[0:Cc, mcol_incl:mcol_incl + Cc], rhs=NL.t[:], start=True, stop=True), [CST.d, NL.d], [pM.d])
                            PE("matmul", dict(out=pM.t[:Cc, 0:192], lhsT=CST.t[0:Cc, mcol_rest:mcol_rest + Cc], rhs=NL.t[:], start=True, stop=True), [CST.d, NL.d], [pM.d])
                            for h in range(4):
                                PE("matmul", dict(out=pM.t[:48, 384 + 2 * h:386 + 2 * h], lhsT=NL.t[:, h * 48:(h + 1) * 48], rhs=ONES32.t[0:Cc, :],
                                                  start=True, stop=True), [NL.d, ONES32.d], [pM.d])
                            A("activation", dict(out=eqt.t[:, 0, :], in_=pM.t[:Cc, 192:384], func=AF.Exp, scale=-1.0 / 16, bias=LNKm.t[0:Cc, 0:1]), [pM.d, LNKm.d], [eqt.d])
                            A("activation", dict(out=eqt.t[:, 1, :], in_=pM.t[:Cc, 192:384], func=AF.Exp, scale=1.0 / 16), [pM.d], [eqt.d])
                            A("activation", dict(out=eqt.t[:, 2, :], in_=pM.t[:Cc, 0:192], func=AF.Exp, scale=-1.0 / 16), [pM.d], [eqt.d])
                            A("activation", dict(out=dect.t[:], in_=pM.t[:48, 384:392], func=AF.Exp, scale=-1.0 / 16), [pM.d], [dect.d])
                            eq, ek, ekh = eqt.t[:, 0, :], eqt.t[:, 1, :], eqt.t[:, 2, :]
                            edep = [eqt.d]
                            cx["dec"] = [dect.t[:, 2 * h:2 * h + 1] for h in range(4)]
                            cx["ddep"] = [dect.d]
                        else:
                            eq, ek, ekh = REQ.t[:, dd, 0, :], REQ.t[:, dd, 1, :], REQ.t[:, dd, 2, :]
                            edep = [REQ.d]
                            cx["dec"] = [RDEC.t[:, dd * 4 + h:dd * 4 + h + 1] for h in range(4)]
                            cx["ddep"] = [RDEC.d]
                        qt = QTs.next()
                        kt_ = KTs.next()
                        kh = KHs.next()
                        vb = VBs.next()
                        qkt = QKTs.next()
                        ptl = PTs.next()
                        po = pOs.next()
                        ps_ = pSs.next()
                        cx.update(qkt=qkt, po=po, ps=ps_)
                        V("tensor_tensor", dict(out=qt.t[:Cc, :], in0=qk.t[:Cc, 0:192], in1=eq, op=ALU.mult), [qk.d] + edep, [qt.d])
                        V("tensor_tensor", dict(out=kt_.t[:Cc, :], in0=qk.t[:Cc, 192:384], in1=ek, op=ALU.mult), [qk.d] + edep, [kt_.d])
                        V("tensor_tensor", dict(out=kh.t[:Cc, :], in0=qk.t[:Cc, 192:384], in1=ekh, op=ALU.mult), [qk.d] + edep, [kh.d])
                        A("activation", dict(out=vb.t[:Cc, :], in_=qk.t[:Cc, 384:768], func=AF.Copy), [qk.d], [vb.d])
                        for h in range(4):
                            PE("transpose", dict(out=pT.t[:48, h, :Cc], in_=qt.t[:Cc, h * 48:(h + 1) * 48], identity=IDB.t[:Cc, :Cc]), [qt.d, IDB.d], [pT.d])
                            PE("transpose", dict(out=pT.t[:48, 4 + h, :Cc], in_=kt_.t[:Cc, h * 48:(h + 1) * 48], identity=IDB.t[:Cc, :Cc]), [kt_.d, IDB.d], [pT.d])
                        V("tensor_copy", dict(out=qkt.t[:, :, :Cc], in_=pT.t[:48, :, :Cc]), [pT.d], [qkt.d])
                        for h in range(4):
                            PE("matmul", dict(out=pA.t[:Cc, h * 128:h * 128 + Cc], lhsT=qkt.t[:, 4 + h, :Cc], rhs=qkt.t[:, h, :Cc], start=True, stop=True), [qkt.d], [pA.d])
                        V("tensor_tensor", dict(out=ptl.t[:Cc, :, :Cc], in0=pA.t[:Cc, :].rearrange("p (h i) -> p h i", h=4)[:, :, :Cc],
                                                in1=MASKB.t[:Cc, dd, :, :Cc], op=ALU.mult), [pA.d, MASKB.d], [ptl.d])
                        for h in range(4):
                            PE("matmul", dict(out=ps_.t[:48, h * 96:(h + 1) * 96], lhsT=kh.t[:Cc, h * 48:(h + 1) * 48], rhs=vb.t[:Cc, h * 96:(h + 1) * 96],
                                              start=True, stop=True), [kh.d, vb.d], [ps_.d])
                        cx["ptl"] = ptl
                        cx["vb"] = vb
                        return cx

                    def back(cx):
                        c, dd = cx["c"], cx["dd"]
                        tok0 = c * Cc
                        qk, qkt, po, ps_, ptl, vb = cx["qk"], cx["qkt"], cx["po"], cx["ps"], cx["ptl"], cx["vb"]
                        sbt = sbh[0]
                        for h in range(4):
                            PE("matmul", dict(out=po.t[:Cc, h * 96:(h + 1) * 96], lhsT=ptl.t[:Cc, h, :Cc], rhs=vb.t[:Cc, h * 96:(h + 1) * 96],
                                              start=True, stop=False), [ptl.d, vb.d], [po.d])
                            PE("matmul", dict(out=po.t[:Cc, h * 96:(h + 1) * 96], lhsT=qkt.t[:, h, :Cc], rhs=sbt.t[:, h, :], start=False, stop=True),
                               [qkt.d, sbt.d], [po.d])
                        for h in range(4):
                            V("scalar_tensor_tensor", dict(out=S32t.t[:, h, :], in0=S32t.t[:, h, :], scalar=cx["dec"][h],
                                                           in1=ps_.t[:48, h * 96:(h + 1) * 96], op0=ALU.mult, op1=ALU.add),
                              [S32t.d, ps_.d] + cx["ddep"], [S32t.d])
                        sbn = SBs.next()
                        A("activation", dict(out=sbn.t[:], in_=S32t.t[:], func=AF.Copy), [S32t.d], [sbn.d])
                        sbh[0] = sbn
                        if dd == 1:
                            os_ = OST.next()
                            A("activation", dict(out=os_.t[:Cc, :], in_=po.t[:Cc, 0:384], func=AF.Copy), [po.d], [os_.d])
                            DMA(OBS[s][tok0:tok0 + Cc, :], os_.t[:Cc, :], r=[os_.d], w=[d_OBS[s]], q="scalar")
                            return
                        ob = cx["ob"]
                        OTf = OT.t[:Cc].rearrange("p h v -> p (h v)")
                        V("tensor_tensor", dict(out=OTf, in0=po.t[:Cc, 0:384], in1=ob.t[:Cc, :], op=ALU.add), [po.d, ob.d], [OT.d])
                        sS1, sS2, sM, sV, sR = ST
                        A("activation", dict(out=SQt.t[:Cc], in_=OT.t[:Cc], func=AF.Square), [OT.d], [SQt.d])
                        V("tensor_reduce", dict(out=sS2.t[:Cc, :], in_=SQt.t[:Cc], axis=AX.X, op=ALU.add), [SQt.d], [sS2.d])
                        if not gla_:
                            V("tensor_reduce", dict(out=sS1.t[:Cc, :], in_=OT.t[:Cc], axis=AX.X, op=ALU.add), [OT.d], [sS1.d])
                            V("tensor_scalar", dict(out=sM.t[:Cc, :], in0=sS1.t[:Cc, :], scalar1=1.0 / 96, scalar2=None, op0=ALU.mult), [sS1.d], [sM.d])
                            V("tensor_tensor", dict(out=sV.t[:Cc, :], in0=sM.t[:Cc, :], in1=sM.t[:Cc, :], op=ALU.mult), [sM.d], [sV.d])
                            V("scalar_tensor_tensor", dict(out=sV.t[:Cc, :], in0=sS2.t[:Cc, :], scalar=1.0 / 96, in1=sV.t[:Cc, :], op0=ALU.mult, op1=ALU.subtract),
                              [sS2.d, sV.d], [sV.d])
                        else:
                            V("tensor_scalar", dict(out=sV.t[:Cc, :], in0=sS2.t[:Cc, :], scalar1=1.0 / 96, scalar2=None, op0=ALU.mult), [sS2.d], [sV.d])
                        A("activation", dict(out=sR.t[:Cc, :], in_=sV.t[:Cc, :], func=AF.Sqrt, bias=EPSm.t[:Cc, 0:1], scale=1.0), [sV.d, EPSm.d], [sR.d])
                        V("reciprocal", dict(out=sR.t[:Cc, :], in_=sR.t[:Cc, :]), [sR.d], [sR.d])
                        for h in range(4):
                            if gla_:
                                V("tensor_scalar", dict(out=YN.t[:Cc, h, :], in0=OT.t[:Cc, h, :], scalar1=sR.t[:Cc, h:h + 1], scalar2=None, op0=ALU.mult),
                                  [OT.d, sR.d], [YN.d])
                            else:
                                V("tensor_scalar", dict(out=YN.t[:Cc, h, :], in0=OT.t[:Cc, h, :], scalar1=sM.t[:Cc, h:h + 1], scalar2=sR.t[:Cc, h:h + 1],
                                                        op0=ALU.subtract, op1=ALU.mult), [OT.d, sM.d, sR.d], [YN.d])
                        YNf = YN.t[:Cc].rearrange("p h v -> p (h v)")
                        V("tensor_tensor", dict(out=YNf, in0=YNf, in1=GNT.t[:Cc, :], op=ALU.mult), [YN.d, GNT.d], [YN.d])
                        V("tensor_tensor", dict(out=YB.t[:Cc, :], in0=YNf, in1=qk.t[:Cc, 768:1152], op=ALU.mult), [YN.d, qk.d], [YB.d])
                        seg = tok0 // 512
                        off = tok0 % 512
                        if off == 0:
                            yfh[0] = yrot.next()
                        yf = yfh[0]
                        for t in range(3):
                            PE("transpose", dict(out=pT2.t[:, t, :Cc], in_=YB.t[:Cc, t * 128:(t + 1) * 128], identity=IDB.t[:Cc, :Cc]), [YB.d, IDB.d], [pT2.d])
                        A("activation", dict(out=yf.t[:, :, off:off + Cc], in_=pT2.t[:, 0:3, :Cc], func=AF.Copy), [pT2.d], [yf.d])
                        if off + Cc == 512 or tok0 + Cc == LT:
                            w_ = off + Cc
                            DMA(YDv[:, :, seg * 512:seg * 512 + w_], yf.t[:, :, 0:w_], r=[yf.d], w=[d_YD[s]], q="scalar")

                    for dd in (1, 0):
                        order = list(range(NCH)) if dd == 0 else list(range(nctx - 1, -1, -1)) + list(range(NCH - 1, nctx - 1, -1))
                        V("memset", dict(ap=S32t.t[:], constant=0.0), w=[S32t.d])
                        sb0 = SBs.next()
                        V("memset", dict(ap=sb0.t[:], constant=0.0), w=[sb0.d])
                        sbh[0] = sb0
                        for c in order:
                            cx = front(c, dd)
                            yield
                            back(cx)
                            yield

                LNKm = mk(mx, "LNKm", [128, 1])
                EPSm = mk(mx, "EPSm", [128, 1])
                G("memset", dict(ap=LNKm.t[:], constant=math.log(48.0 ** -0.5)), w=[LNKm.d])
                G("memset", dict(ap=EPSm.t[:], constant=EPS), w=[EPSm.d])
                for s in range(nseq):
                    if FLAGS.get("s5", True):
                        s5_pre(s)
                    gens = []
                    if FLAGS.get("ret", True):
                        gens.append(attn(s, "ret"))
                    if FLAGS.get("gla", True):
                        gens.append(attn(s, "gla"))
                    while gens:
                        for g_ in list(gens):
                            try:
                                next(g_)
                            except StopIteration:
                                gens.remove(g_)
                    if FLAGS.get("s5", True):
                        s5_post(s)
                S.flush()
            if upto == "mix":
                break


            last = (l == depth - 1)
            HH = HHs
            with contextlib.ExitStack() as ph:
                WBS = mk(ph, "WBS", [128, 2, D], BF16)
                WBR = mk(ph, "WBR", [128, 3, D], BF16)
                WBG = mk(ph, "WBG", [128, 3, D], BF16)
                WO = mk(ph, "WO", [128, 8, D], BF16)
                dws = [Dep() for _ in range(2)]
                dwr = [Dep() for _ in range(3)]
                dwg = [Dep() for _ in range(3)]
                dwo = [Dep() for _ in range(8)]
                stg = Rot([mk(ph, "stg%d" % i, [128, 1024]) for i in range(2)])
                engs = ["gpsimd", "vector", "scalar"]
                load_weight(WBS, dws, w_br_s5[l], 2, D, stg, engs, 1024)
                load_weight(WBR, dwr, w_br_ret[l], 3, D, stg, engs, 1024)
                load_weight(WBG, dwg, w_br_gla[l], 3, D, stg, engs, 1024)
                load_weight(WO, dwo, w_out[l], 8, D, stg, engs, 1024)
                xts = Rot([mk(ph, "xt%d" % i, [128, 8, 512]) for i in range(2)])
                yss = Rot([mk(ph, "ys%d" % i, [128, 8, 512], BF16) for i in range(2)])
                gts = Rot([mk(ph, "gt%d" % i, [128, 24, 512], BF16) for i in range(2)])
                mTs = Rot([mk(ph, "mT%d" % i, [128, 8, 512], BF16) for i in range(2)])
                m1s = Rot([mk(ph, "m1_%d" % i, [128, 512], BF16) for i in range(2)])
                m2s = Rot([mk(ph, "m2_%d" % i, [128, 512], BF16) for i in range(3)])
                pp4 = Rot([mkp(ph, "pq%d" % i, [128, 512]) for i in range(7)])
                for s in range(nseq):
                    for (t0, T) in TOK_TILES:
                        if last and t0 < CTXL:
                            continue
                        col = 4 if t0 < CTXL else s
                        xt = xts.next()
                        ys = yss.next()
                        gt = gts.next()
                        mT = mTs.next()
                        DMA(xt.t[:, :, :T], XIN[s][:, t0:t0 + T].rearrange("(k p) t -> p k t", p=128), r=[d_XIN[s]], w=[xt.d])
                        DMA(ys.t[:, 0:2, :T], YS5[s][:, t0:t0 + T].rearrange("(c p) t -> p c t", p=128), r=[d_YS5[s]], w=[ys.d])
                        DMA(ys.t[:, 2:5, :T], YRET[s][:, t0:t0 + T].rearrange("(c p) t -> p c t", p=128), r=[d_YRET[s]], w=[ys.d])
                        DMA(ys.t[:, 5:8, :T], YGLA[s][:, t0:t0 + T].rearrange("(c p) t -> p c t", p=128), r=[d_YGLA[s]], w=[ys.d])
                        DMA(gt.t[:, :, :T], GATE[s][:, t0:t0 + T].rearrange("(m p) t -> p m t", p=128), r=[d_GATE[s]], w=[gt.d])
                        for dt in range(8):
                            cs = slice(dt * 128, (dt + 1) * 128)
                            m1 = m1s.next()
                            specs = [(WBS, dws, 0, 2, 0), (WBR, dwr, 2, 3, 8), (WBG, dwg, 5, 3, 16)]
                            for bi, (Wt, dw, y0, nk, g0_) in enumerate(specs):
                                p = pp4.next()
                                for kt in range(nk):
                                    PE("matmul", dict(out=p.t[:, :T], lhsT=Wt.t[:, kt, cs], rhs=ys.t[:, y0 + kt, :T], start=(kt == 0), stop=(kt == nk - 1)),
                                       [dw[kt], ys.d], [p.d])
                                if bi == 0:
                                    V("tensor_tensor", dict(out=m1.t[:, :T], in0=p.t[:, :T], in1=gt.t[:, g0_ + dt, :T], op=ALU.mult), [p.d, gt.d], [m1.d])
                                else:
                                    m2 = m2s.next()
                                    V("tensor_tensor", dict(out=m2.t[:, :T], in0=p.t[:, :T], in1=gt.t[:, g0_ + dt, :T], op=ALU.mult), [p.d, gt.d], [m2.d])
                                    if bi == 1:
                                        V("tensor_tensor", dict(out=m1.t[:, :T], in0=m1.t[:, :T], in1=m2.t[:, :T], op=ALU.add), [m1.d, m2.d], [m1.d])
                                    else:
                                        V("tensor_tensor", dict(out=mT.t[:, dt, :T], in0=m1.t[:, :T], in1=m2.t[:, :T], op=ALU.add), [m1.d, m2.d], [mT.d])
                        for dt in range(8):
                            cs = slice(dt * 128, (dt + 1) * 128)
                            p = pp4.next()
                            for k in range(8):
                                PE("matmul", dict(out=p.t[:, :T], lhsT=WO.t[:, k, cs], rhs=mT.t[:, k, :T], start=(k == 0), stop=(k == 7)), [dwo[k], mT.d], [p.d])
                            V("scalar_tensor_tensor", dict(out=xt.t[:, dt, :T], in0=p.t[:, :T], scalar=MOD.t[:, l, 16 + dt, col:col + 1], in1=xt.t[:, dt, :T],
                                                           op0=ALU.mult, op1=ALU.add), [p.d, MOD.d, xt.d], [xt.d])
                        DMA(XMID[s][:, t0:t0 + T].rearrange("(k p) t -> p k t", p=128), xt.t[:, :, :T], r=[xt.d], w=[d_XMID[s]], q="gpsimd")
                S.flush()
            if upto == "ph4a":
                break

            with contextlib.ExitStack() as ph:
                WF1 = mk(ph, "WF1", [128, 8, 2 * FFH], BF16)
                dw1 = [Dep() for _ in range(8)]
                stg = Rot([mk(ph, "stg%d" % i, [128, 512]) for i in range(2)])
                engs = ["gpsimd", "vector", "scalar"]
                load_weight(WF1, dw1, w_ffn_in[l], 8, 2 * FFH, stg, engs, 512)
                xts = Rot([mk(ph, "xt%d" % i, [128, 8, 512]) for i in range(2)])
                sq = mk(ph, "sq", [128, 8, 512], BF16)
                xn = mk(ph, "xn", [128, 8, 512], BF16)
                rs = mk(ph, "rs", [128, 512])
                tmps = Rot([mk(ph, "tmp%d" % i, [128, 512]) for i in range(2)])
                hst = Rot([mk(ph, "hst%d" % i, [128, 11, 512], BF16) for i in range(2)])
                sas = Rot([mk(ph, "sa%d" % i, [128, 512]) for i in range(2)])
                pms = mkp(ph, "pms", [128, 512])
                pp4 = Rot([mkp(ph, "pq%d" % i, [128, 512]) for i in range(6)])
                for s in range(nseq):
                    for (t0, T) in TOK_TILES:
                        if last and t0 < CTXL:
                            continue
                        col = 4 if t0 < CTXL else s
                        xt = xts.next()
                        DMA(xt.t[:, :, :T], XMID[s][:, t0:t0 + T].rearrange("(k p) t -> p k t", p=128), r=[d_XMID[s]], w=[xt.d])
                        norm_mod(xt, T, 1, 24, l, col, sq, pms, rs, tmps, xn)
                        hs_ = None
                        for ft in range(22):
                            if ft % 11 == 0:
                                hs_ = hst.next()
                            pa = pp4.next()
                            pb = pp4.next()
                            for k in range(8):
                                PE("matmul", dict(out=pa.t[:, :T], lhsT=WF1.t[:, k, ft * 128:(ft + 1) * 128], rhs=xn.t[:, k, :T], start=(k == 0), stop=(k == 7)),
                                   [dw1[k], xn.d], [pa.d])
                            for k in range(8):
                                PE("matmul", dict(out=pb.t[:, :T], lhsT=WF1.t[:, k, FFH + ft * 128:FFH + (ft + 1) * 128], rhs=xn.t[:, k, :T], start=(k == 0), stop=(k == 7)),
                                   [dw1[k], xn.d], [pb.d])
                            sa = sas.next()
                            A("activation", dict(out=sa.t[:, :T], in_=pa.t[:, :T], func=AF.Silu), [pa.d], [sa.d])
                            V("tensor_tensor", dict(out=hs_.t[:, ft % 11, :T], in0=pb.t[:, :T], in1=sa.t[:, :T], op=ALU.mult), [pb.d, sa.d], [hs_.d])
                            if ft % 11 == 10:
                                f0 = (ft // 11) * 11
                                DMA(HH[s][f0 * 128:(f0 + 11) * 128, t0:t0 + T].rearrange("(f p) t -> p f t", p=128), hs_.t[:, :, :T], r=[hs_.d], w=[d_HH[s]], q="gpsimd")
                S.flush()

            with contextlib.ExitStack() as ph:
                WF2 = mk(ph, "WF2", [128, 22, D], BF16)
                dw2 = [Dep() for _ in range(22)]
                stg = Rot([mk(ph, "stg%d" % i, [128, 1024]) for i in range(2)])
                engs = ["gpsimd", "vector"]
                load_weight(WF2, dw2, w_ffn_out[l], 22, D, stg, engs, 1024)
                xts = Rot([mk(ph, "xt%d" % i, [128, 8, 512]) for i in range(2)])
                hhs = Rot([mk(ph, "hh%d" % i, [128, 22, 512], BF16) for i in range(2)])
                sq = mk(ph, "sq", [128, 8, 512], BF16)
                rs = mk(ph, "rs", [128, 512])
                pms = mkp(ph, "pms", [128, 512])
                pp4 = Rot([mkp(ph, "pq%d" % i, [128, 512]) for i in range(6)])
                EPSf = mk(ph, "EPSf", [128, 1])
                G("memset", dict(ap=EPSf.t[:], constant=EPS), w=[EPSf.d])
                for s in range(nseq):
                    for (t0, T) in TOK_TILES:
                        if last and t0 < CTXL:
                            continue
                        col = 4 if t0 < CTXL else s
                        xt = xts.next()
                        hh = hhs.next()
                        DMA(xt.t[:, :, :T], XMID[s][:, t0:t0 + T].rearrange("(k p) t -> p k t", p=128), r=[d_XMID[s]], w=[xt.d])
                        DMA(hh.t[:, :, :T], HH[s][:, t0:t0 + T].rearrange("(f p) t -> p f t", p=128), r=[d_HH[s]], w=[hh.d])
                        for dt in range(8):
                            cs = slice(dt * 128, (dt + 1) * 128)
                            p = pp4.next()
                            for ft in range(22):
                                PE("matmul", dict(out=p.t[:, :T], lhsT=WF2.t[:, ft, cs], rhs=hh.t[:, ft, :T], start=(ft == 0), stop=(ft == 21)), [dw2[ft], hh.d], [p.d])
                            V("scalar_tensor_tensor", dict(out=xt.t[:, dt, :T], in0=p.t[:, :T], scalar=MOD.t[:, l, 40 + dt, col:col + 1], in1=xt.t[:, dt, :T],
                                                           op0=ALU.mult, op1=ALU.add), [p.d, MOD.d, xt.d], [xt.d])
                        if not last:
                            DMA(X1[s][:, t0:t0 + T].rearrange("(k p) t -> p k t", p=128), xt.t[:, :, :T], r=[xt.d], w=[d_X1[s]], q="gpsimd")
                        else:
                            V("tensor_tensor", dict(out=sq.t[:, :, :T], in0=xt.t[:, :, :T], in1=xt.t[:, :, :T], op=ALU.mult), [xt.d], [sq.d])
                            for k in range(8):
                                PE("matmul", dict(out=pms.t[:, :T], lhsT=ONESB.t[:], rhs=sq.t[:, k, :T], start=(k == 0), stop=(k == 7)), [ONESB.d, sq.d], [pms.d])
                            A("activation", dict(out=rs.t[:, :T], in_=pms.t[:, :T], func=AF.Ln, bias=EPSf.t[:, 0:1], scale=1.0), [pms.d, EPSf.d], [rs.d])
                            A("activation", dict(out=rs.t[:, :T], in_=rs.t[:, :T], func=AF.Exp, scale=-0.5), [rs.d], [rs.d])
                            for k in range(8):
                                V("scalar_tensor_tensor", dict(out=xt.t[:, k, :T], in0=xt.t[:, k, :T], scalar=NRM.t[:, 2, 0, k:k + 1], in1=rs.t[:, :T],
                                                               op0=ALU.mult, op1=ALU.mult), [xt.d, NRM.d, rs.d], [xt.d])
                            DMA(outT[s][:, t0 - CTXL:t0 - CTXL + T].rearrange("(k p) t -> p k t", p=128), xt.t[:, :, :T], r=[xt.d], w=[d_OUT], q="gpsimd")
                S.flush()

    return nc


def _consts():
    c = np.zeros((128, NCST), np.float32)
    j = np.arange(128)[:, None]
    i = np.arange(128)[None, :]
    c[:, C_ID:C_ID + 128] = (j == i)
    c[:, C_MF:C_MF + 128] = (j <= i)
    c[:, C_MB:C_MB + 128] = (j >= i)
    c[:, C_MFS:C_MFS + 128] = (j < i)
    c[:, C_MBS:C_MBS + 128] = (j > i)
    c[:, C_IOTA] = np.arange(128)
    nf = 12
    inv = (1.0 / (np.float32(10000.0) ** (np.arange(nf, dtype=np.float32) / np.float32(nf)))).astype(np.float32)
    t = (np.arange(16)[None, :] * 128 + np.arange(128)[:, None]).astype(np.float32)
    r = np.floor(t / 64.0).astype(np.float32)
    cc = (t - r * 64.0).astype(np.float32)
    ang = np.concatenate([r[:, :, None] * inv[None, None, :], cc[:, :, None] * inv[None, None, :]], -1).astype(np.float32)
    c[:, C_COS:C_COS + 384] = np.cos(ang).reshape(128, 384)
    c[:, C_SIN:C_SIN + 384] = np.sin(ang).reshape(128, 384)
    nv = np.zeros((32, 5, 8), np.float32)
    idx = np.arange(8, dtype=np.float32)
    for gd in range(32):
        if gd < 16:
            nv[gd, 0] = -idx
            nv[gd, 1] = idx
            nv[gd, 2] = 7 - idx
            nv[gd, 3] = idx + 1
        else:
            nv[gd, 0] = idx - 7
            nv[gd, 1] = 7 - idx
            nv[gd, 2] = idx
            nv[gd, 3] = 8 - idx
        nv[gd, 4, 0] = 1
        nv[gd, 4, 1] = 8
    c[:, C_NV:C_NV + 1280] = nv.reshape(1, 1280)
    jj = (np.arange(128) // 16)[:, None]
    ii = (np.arange(128) // 16)[None, :]
    c[:, C_MGF:C_MGF + 128] = (ii >= jj)
    c[:, C_MGB:C_MGB + 128] = (ii <= jj)
    return c


def host_prep(inputs, core, nseq=NSEQ):
    f = lambda a: np.ascontiguousarray(a, dtype=np.float32)
    b0 = core * nseq
    x = inputs["x"][b0:b0 + nseq]
    ctx = inputs["ctx"][b0:b0 + nseq]
    xt = np.concatenate([ctx, x], axis=1).transpose(0, 2, 1)
    ct = np.zeros((D, 8), np.float32)
    ct[:, :nseq] = inputs["c"][b0:b0 + nseq].T
    ct[:, 4] = inputs["c_ctx"]
    m = {
        "xT": f(xt), "cT": ct, "cst": _consts(),
        "s5lam": f(np.stack([inputs["s5_lam_re"], inputs["s5_lam_im"]], 1).transpose(0, 4, 1, 2, 3).reshape(DEPTH, 64, 2, 32)),
        "s5dt": f(inputs["s5_log_dt"].reshape(DEPTH, 32)),
        "s5B": f(np.stack([inputs["s5_b_re"], inputs["s5_b_im"]], 1).transpose(0, 3, 1, 2, 4)),
        "s5C": f(np.stack([inputs["s5_c_re"], inputs["s5_c_im"]], 1).transpose(0, 4, 1, 2, 3)),
        "ret_ld": f(inputs["ret_log_decay"].reshape(DEPTH, 8)),
        "gla_gw": f(inputs["gla_gate_w"]), "gla_gb": f(inputs["gla_gate_b"]),
    }
    for k in ["w_mod", "b_mod", "norm_mix", "norm_ffn", "w_in", "s5_d", "s5_glu_w", "s5_glu_b", "ret_gn", "gla_norm",
              "w_br_s5", "w_br_ret", "w_br_gla", "w_out", "w_ffn_in", "w_ffn_out", "norm_final"]:
        m[k] = f(inputs[k])
    return m


_NC_CACHE = {}


def kernel(**inputs):
    if "nc" not in _NC_CACHE:
        _NC_CACHE["nc"] = build()
    nc = _NC_CACHE["nc"]
    in_maps = [host_prep(inputs, c) for c in range(8)]
    res = run_bass_kernel_spmd(nc, in_maps, core_ids=list(range(8)))
    outs = [np.asarray(r["outT"]).transpose(0, 2, 1) for r in res.results]
    return np.ascontiguousarray(np.concatenate(outs, axis=0), dtype=np.float32)
```

```python
import math
import contextlib
import numpy as np
import concourse.bass as bass
import concourse.mybir as mybir
from concourse.bass_utils import run_bass_kernel_spmd

F32 = mybir.dt.float32
BF16 = mybir.dt.bfloat16
ALU = mybir.AluOpType
AF = mybir.ActivationFunctionType
AX = mybir.AxisListType

D = 1024
SEQ = 2048
CTXL = 256
LT = SEQ + CTXL
DEPTH = 2
NSEQ = 4
INDIM = 5664
FFH = 2816
EPS = 1e-6
NTM = 2560
NDS = 60
DPOOL = {"sync": (0, 24), "gpsimd": (24, 48), "scalar": (48, 60)}
FLAGS = {}

C_ID = 0
C_MF = 128
C_MB = 256
C_MFS = 384
C_MBS = 512
C_IOTA = 640
C_COS = 641
C_SIN = C_COS + 384
C_NV = C_SIN + 384
C_MGF = C_NV + 1280
C_MGB = C_MGF + 128
NCST = C_MGB + 128


class Dep:
    __slots__ = ("w", "r")

    def __init__(self):
        self.w = None
        self.r = []


class Tl:
    def __init__(self, t):
        self.t = t
        self.d = Dep()


class Sched:
    CE = ["tensor", "vector", "scalar", "gpsimd"]
    ENG = ["tensor", "vector", "scalar", "gpsimd", "sync"]

    def __init__(self, nc, stack):
        self.nc = nc
        self.ops = {e: [] for e in self.ENG}
        self.sem = {e: stack.enter_context(nc.semaphore("s_" + e)) for e in self.CE}
        self.cnt = {e: 0 for e in self.CE}
        self.seen = {e: {} for e in self.ENG}
        self.dsem = [stack.enter_context(nc.semaphore("d%d" % i)) for i in range(NDS)]
        self.dcnt = [0] * NDS
        self.dnext = {q: lo for q, (lo, hi) in DPOOL.items()}
        self.nins = 0

    def _semof(self, key):
        return self.sem[key[1]] if key[0] == "e" else self.dsem[key[1]]

    def _collect(self, eng, reads, writes):
        need = {}
        for d in reads:
            if d.w is not None:
                k, v = d.w
                if need.get(k, 0) < v:
                    need[k] = v
        for d in writes:
            if d.w is not None:
                k, v = d.w
                if need.get(k, 0) < v:
                    need[k] = v
            for (k, v) in d.r:
                if need.get(k, 0) < v:
                    need[k] = v
        waits = []
        seen = self.seen[eng]
        for k, v in need.items():
            if eng == "tensor" and k == ("e", "tensor"):
                continue
            if seen.get(k, 0) >= v:
                continue
            seen[k] = v
            waits.append((self._semof(k), v))
        return waits

    def _mark(self, tok, reads, writes):
        ws = set()
        for d in writes:
            d.w = tok
            d.r = []
            ws.add(id(d))
        for d in reads:
            if id(d) not in ws:
                d.r.append(tok)
                if len(d.r) > 48:
                    m = {}
                    for (k, v) in d.r:
                        if m.get(k, 0) < v:
                            m[k] = v
                    d.r = list(m.items())

    def op(self, eng, fn, reads=(), writes=()):
        waits = self._collect(eng, reads, writes)
        self.cnt[eng] += 1
        val = self.cnt[eng]
        sem = self.sem[eng]

        def run(e, fn=fn, waits=waits, sem=sem):
            for (s, v) in waits:
                e.wait_ge(s, v)
            fn(e).then_inc(sem, 1)

        self.ops[eng].append(run)
        self.nins += 1
        self._mark((("e", eng), val), reads, writes)

    def dma(self, q, out, in_, reads=(), writes=(), **kw):
        waits = self._collect(q, reads, writes)
        lo, hi = DPOOL[q]
        i = self.dnext[q]
        self.dnext[q] = lo + (i + 1 - lo) % (hi - lo)
        prev = self.dcnt[i]
        self.dcnt[i] += 16
        val = self.dcnt[i]
        key = ("d", i)
        if prev > 0 and self.seen[q].get(key, 0) < prev:
            waits.append((self.dsem[i], prev))
            self.seen[q][key] = prev
        dsem = self.dsem[i]

        def run(e, waits=waits, dsem=dsem, out=out, in_=in_, kw=kw):
            for (s, v) in waits:
                e.wait_ge(s, v)
            e.dma_start(out=out, in_=in_, **kw).then_inc(dsem, 16)

        self.ops[q].append(run)
        self.nins += 1
        self._mark((key, val), reads, writes)

    def flush(self):
        nc = self.nc
        finals = [(self.dsem[i], self.dcnt[i]) for i in range(NDS) if self.dcnt[i] > 0]
        finals += [(self.sem[e], self.cnt[e]) for e in self.CE if self.cnt[e] > 0]

        def fin(e, finals=finals):
            for (s, v) in finals:
                e.wait_ge(s, v)

        self.ops["sync"].append(fin)
        ops = self.ops
        self.ops = {e: [] for e in self.ENG}
        with nc.Block() as block:
            @block.tensor
            def _(e):
                for f in ops["tensor"]:
                    f(e)

            @block.vector
            def _(e):
                for f in ops["vector"]:
                    f(e)

            @block.scalar
            def _(e):
                for f in ops["scalar"]:
                    f(e)

            @block.gpsimd
            def _(e):
                for f in ops["gpsimd"]:
                    f(e)

            @block.sync
            def _(e):
                for f in ops["sync"]:
                    f(e)
        for e in self.ENG:
            for i in range(NDS):
                self.seen[e][("d", i)] = self.dcnt[i]
            for c in self.CE:
                self.seen[e][("e", c)] = self.cnt[c]


class Rot:
    def __init__(self, items):
        self.items = items
        self.i = 0

    def next(self):
        x = self.items[self.i % len(self.items)]
        self.i += 1
        return x


def bc(ap, shape, axis):
    return ap.unsqueeze(axis).broadcast_to(list(shape))


def build(nseq=NSEQ, depth=DEPTH, debug=False, upto="all"):
    nc = bass.Bass("TRN2", target_bir_lowering=False)

    def din(name, shape, dt=F32):
        return nc.dram_tensor(name, list(shape), dt, kind="ExternalInput").ap()

    skind = "ExternalOutput" if debug else "Internal"

    def scr(name, shape, dt=F32):
        return nc.dram_tensor(name, list(shape), dt, kind=skind).ap()

    xT = din("xT", [nseq, D, LT])
    cT = din("cT", [D, 8])
    cst = din("cst", [128, NCST])
    w_mod = din("w_mod", [DEPTH, D, 6 * D])
    b_mod = din("b_mod", [DEPTH, 6 * D])
    norm_mix = din("norm_mix", [DEPTH, D])
    norm_ffn = din("norm_ffn", [DEPTH, D])
    w_in = din("w_in", [DEPTH, D, INDIM])
    s5lam = din("s5lam", [DEPTH, 64, 2, 32])
    s5dt = din("s5dt", [DEPTH, 32])
    s5B = din("s5B", [DEPTH, 64, 2, 16, 16])
    s5C = din("s5C", [DEPTH, 64, 2, 16, 16])
    s5_d = din("s5_d", [DEPTH, 256])
    s5_glu_w = din("s5_glu_w", [DEPTH, 256, 256])
    s5_glu_b = din("s5_glu_b", [DEPTH, 256])
    ret_ld = din("ret_ld", [DEPTH, 8])
    ret_gn = din("ret_gn", [DEPTH, 384])
    gla_gw = din("gla_gw", [DEPTH, 2, 16, 192])
    gla_gb = din("gla_gb", [DEPTH, 2, 192])
    gla_norm = din("gla_norm", [DEPTH, 384])
    w_br_s5 = din("w_br_s5", [DEPTH, 256, D])
    w_br_ret = din("w_br_ret", [DEPTH, 384, D])
    w_br_gla = din("w_br_gla", [DEPTH, 384, D])
    w_out = din("w_out", [DEPTH, D, D])
    w_ffn_in = din("w_ffn_in", [DEPTH, D, 2 * FFH])
    w_ffn_out = din("w_ffn_out", [DEPTH, FFH, D])
    norm_final = din("norm_final", [D])
    outT = nc.dram_tensor("outT", [nseq, D, SEQ], F32, kind="ExternalOutput").ap()

    TM = scr("TM", [nseq, LT, NTM])
    ZF = scr("ZF", [nseq, 32, LT])
    GATE = scr("GATE", [nseq, 3 * D, LT], BF16)
    OBR = scr("OBR", [nseq, LT, 384])
    OBG = scr("OBG", [nseq, LT, 384])
    YS5 = scr("YS5", [nseq, 256, LT], BF16)
    YRET = scr("YRET", [nseq, 384, LT], BF16)
    YGLA = scr("YGLA", [nseq, 384, LT], BF16)
    XMID = scr("XMID", [nseq, D, LT])
    X1 = scr("X1", [nseq, D, LT])
    HHs = scr("HH", [nseq, FFH, LT], BF16)
    d_HH = [Dep() for _ in range(nseq)]
    d_TM = [Dep() for _ in range(nseq)]
    d_ZF = [Dep() for _ in range(nseq)]
    d_GATE = [Dep() for _ in range(nseq)]
    d_OBR = [Dep() for _ in range(nseq)]
    d_OBG = [Dep() for _ in range(nseq)]
    d_YS5 = [Dep() for _ in range(nseq)]
    d_YRET = [Dep() for _ in range(nseq)]
    d_YGLA = [Dep() for _ in range(nseq)]
    d_XMID = [Dep() for _ in range(nseq)]
    d_X1 = [Dep() for _ in range(nseq)]
    d_OUT = Dep()

    top = contextlib.ExitStack()
    with top:
        S = Sched(nc, top)

        def OP(eng, name, kw, r=(), w=()):
            S.op(eng, lambda e, name=name, kw=kw: getattr(e, name)(**kw), r, w)

        def V(name, kw, r=(), w=()):
            OP("vector", name, kw, r, w)

        def A(name, kw, r=(), w=()):
            OP("scalar", name, kw, r, w)

        def G(name, kw, r=(), w=()):
            OP("gpsimd", name, kw, r, w)

        def PE(name, kw, r=(), w=()):
            OP("tensor", name, kw, r, w)

        def DMA(out, in_, r=(), w=(), q="sync", **kw):
            S.dma(q, out, in_, r, w, **kw)

        uid = [0]

        def mk(stack, name, shape, dt=F32):
            uid[0] += 1
            return Tl(stack.enter_context(nc.sbuf_tensor("%s_%d" % (name, uid[0]), list(shape), dt)))

        def mkp(stack, name, shape, dt=F32):
            uid[0] += 1
            return Tl(stack.enter_context(nc.psum_tensor("%s_%d" % (name, uid[0]), list(shape), dt)))

        CST = mk(top, "CST", [128, NCST])
        MOD = mk(top, "MOD", [128, DEPTH, 48, 8])
        GN12 = mk(top, "GN12", [128, DEPTH, 2, 8, 8])
        NRM = mk(top, "NRM", [128, 3, 2, 8])
        IDB = mk(top, "IDB", [128, 128], BF16)
        ONESB = mk(top, "ONESB", [128, 128], BF16)
        ONES32 = mk(top, "ONES32", [128, 2])
        MASKB = mk(top, "MASKB", [128, 2, 4, 128], BF16)

        DMA(CST.t[:], cst, w=[CST.d])
        V("tensor_copy", dict(out=IDB.t[:], in_=CST.t[:, C_ID:C_ID + 128]), [CST.d], [IDB.d])
        G("memset", dict(ap=ONESB.t[:], constant=1.0 / 1024.0), w=[ONESB.d])
        G("memset", dict(ap=ONES32.t[:], constant=1.0), w=[ONES32.d])
        for dd, co in ((0, C_MF), (1, C_MB)):
            for h in range(4):
                V("tensor_copy", dict(out=MASKB.t[:, dd, h, :], in_=CST.t[:, co:co + 128]), [CST.d], [MASKB.d])
        for l in range(DEPTH):
            DMA(NRM.t[:, 0, l, :], norm_mix[l].rearrange("(k p) -> p k", p=128), w=[NRM.d], allow_slow_non_contiguous=True)
            DMA(NRM.t[:, 1, l, :], norm_ffn[l].rearrange("(k p) -> p k", p=128), w=[NRM.d], allow_slow_non_contiguous=True)
        DMA(NRM.t[:, 2, 0, :], norm_final.rearrange("(k p) -> p k", p=128), w=[NRM.d], allow_slow_non_contiguous=True)

        with contextlib.ExitStack() as ph:
            CTt = mk(ph, "CTt", [128, 8, 8])
            SC = mk(ph, "SC", [128, 8, 8])
            BM = mk(ph, "BM", [128, 48])
            wst = Rot([mk(ph, "wms%d" % i, [128, 8, 512]) for i in range(2)])
            pm = Rot([mkp(ph, "pm%d" % i, [128, 512]) for i in range(2)])
            DMA(CTt.t[:], cT.rearrange("(k p) c -> p k c", p=128), w=[CTt.d])
            A("activation", dict(out=SC.t[:], in_=CTt.t[:], func=AF.Silu), [CTt.d], [SC.d])
            for l in range(depth):
                DMA(BM.t[:], b_mod[l].rearrange("(t p) -> p t", p=128), w=[BM.d], allow_slow_non_contiguous=True)
                for fb in range(12):
                    ws = wst.next()
                    DMA(ws.t[:], w_mod[l][:, fb * 512:(fb + 1) * 512].rearrange("(k p) f -> p k f", p=128), w=[ws.d])
                    for j in range(4):
                        t = fb * 4 + j
                        p = pm.next()
                        for k in range(8):
                            PE("matmul", dict(out=p.t[:, 0:8], lhsT=ws.t[:, k, j * 128:(j + 1) * 128],
                                                                         rhs=SC.t[:, k, :], start=(k == 0), stop=(k == 7)), [ws.d, SC.d], [p.d])
                        V("tensor_scalar", dict(out=MOD.t[:, l, t, :], in0=p.t[:, 0:8], scalar1=BM.t[:, t:t + 1],
                                                                  scalar2=None, op0=ALU.add), [p.d, BM.d], [MOD.d])
                for k in range(8):
                    V("tensor_scalar", dict(out=GN12.t[:, l, 0, k, :], in0=MOD.t[:, l, 8 + k, :], scalar1=1.0,
                                                          scalar2=NRM.t[:, 0, l, k:k + 1], op0=ALU.add, op1=ALU.mult), [MOD.d, NRM.d], [GN12.d])
                    V("tensor_scalar", dict(out=GN12.t[:, l, 1, k, :], in0=MOD.t[:, l, 32 + k, :], scalar1=1.0,
                                                          scalar2=NRM.t[:, 1, l, k:k + 1], op0=ALU.add, op1=ALU.mult), [MOD.d, NRM.d], [GN12.d])
            S.flush()

        def load_weight(dst, ddeps, src, K, N, stg, engs, chunk=2048):
            for k in range(K):
                for c0 in range(0, N, 2048):
                    w_ = min(2048, N - c0)
                    DMA(dst.t[:, k, c0:c0 + w_], src[k * 128:(k + 1) * 128, c0:c0 + w_], w=[ddeps[k]], q="gpsimd")

        def norm_mod(xt, T, gsel, shift_t0, l, col, sq, pms, rs, tmps, xn):
            A("activation", dict(out=sq.t[:, :, :T], in_=xt.t[:, :, :T], func=AF.Square), [xt.d], [sq.d])
            for k in range(8):
                PE("matmul", dict(out=pms.t[:, :T], lhsT=ONESB.t[:], rhs=sq.t[:, k, :T], start=(k == 0), stop=(k == 7)), [ONESB.d, sq.d], [pms.d])
            A("activation", dict(out=rs.t[:, :T], in_=pms.t[:, :T], func=AF.Sqrt, bias=EPS, scale=1.0), [pms.d], [rs.d])
            V("reciprocal", dict(out=rs.t[:, :T], in_=rs.t[:, :T]), [rs.d], [rs.d])
            for k in range(8):
                tm_ = tmps.next()
                V("scalar_tensor_tensor", dict(out=tm_.t[:, :T], in0=xt.t[:, k, :T],
                                                                 scalar=GN12.t[:, l, gsel, k, col:col + 1], in1=rs.t[:, :T],
                                                                 op0=ALU.mult, op1=ALU.mult), [xt.d, GN12.d, rs.d], [tm_.d])
                A("activation", dict(out=xn.t[:, k, :T], in_=tm_.t[:, :T], func=AF.Identity,
                                                       bias=MOD.t[:, l, shift_t0 + k, col:col + 1], scale=1.0), [tm_.d, MOD.d], [xn.d])

        TOK_TILES = [(0, 256)] + [(256 + i * 512, 512) for i in range(4)]

        for l in range(depth):
            XIN = xT if l == 0 else X1
            d_XIN = [Dep() for _ in range(nseq)] if l == 0 else d_X1

            with contextlib.ExitStack() as ph:
                WIN = mk(ph, "WIN", [128, 8, INDIM], BF16)
                d_win = [Dep() for _ in range(8)]
                stg = None
                xts = Rot([mk(ph, "xt%d" % i, [128, 8, 512]) for i in range(2)])
                sq = mk(ph, "sq", [128, 8, 512], BF16)
                xn = mk(ph, "xn", [128, 8, 512], BF16)
                rs = mk(ph, "rs", [128, 512])
                tmps = Rot([mk(ph, "tmp%d" % i, [128, 512]) for i in range(2)])
                gst = Rot([mk(ph, "gst%d" % i, [128, 6, 512], BF16) for i in range(2)])
                zst = Rot([mk(ph, "zst%d" % i, [32, 512]) for i in range(2)])
                tmst = Rot([mk(ph, "tmst%d" % i, [128, NTM]) for i in range(2)])
                pms = mkp(ph, "pms", [128, 512])
                pf = Rot([mkp(ph, "pf%d" % i, [128, 512]) for i in range(3)])
                pt = Rot([mkp(ph, "pt%d" % i, [128, 512]) for i in range(3)])
                load_weight(WIN, d_win, w_in[l], 8, INDIM, stg, ["gpsimd", "vector", "scalar"], 512)
                for s in range(nseq):
                    for (t0, T) in TOK_TILES:
                        col = 4 if t0 < CTXL else s
                        xt = xts.next()
                        DMA(xt.t[:, :, :T], XIN[s][:, t0:t0 + T].rearrange("(k p) t -> p k t", p=128), r=[d_XIN[s]], w=[xt.d])
                        norm_mod(xt, T, 0, 0, l, col, sq, pms, rs, tmps, xn)
                        gs = None
                        for mt in range(25):
                            if (l == depth - 1) and t0 < CTXL and mt < 24:
                                continue
                            if mt % 6 == 0 and mt < 24:
                                gs = gst.next()
                            p = pf.next()
                            c0 = 2592 + mt * 128 if mt < 24 else 2560
                            M = 128 if mt < 24 else 32
                            for k in range(8):
                                PE("matmul", dict(out=p.t[:M, :T], lhsT=WIN.t[:, k, c0:c0 + M], rhs=xn.t[:, k, :T],
                                                                             start=(k == 0), stop=(k == 7)), [d_win[k], xn.d], [p.d])
                            if mt < 24:
                                A("activation", dict(out=gs.t[:, mt % 6, :T], in_=p.t[:, :T], func=AF.Sigmoid), [p.d], [gs.d])
                                if mt % 6 == 5:
                                    m0 = (mt // 6) * 6
                                    DMA(GATE[s][m0 * 128:(m0 + 6) * 128, t0:t0 + T].rearrange("(m p) t -> p m t", p=128), gs.t[:, :, :T], r=[gs.d], w=[d_GATE[s]], q="scalar")
                            else:
                                zs = zst.next()
                                V("tensor_copy", dict(out=zs.t[:, :T], in_=p.t[:32, :T]), [p.d], [zs.d])
                                DMA(ZF[s][:, t0:t0 + T], zs.t[:, :T], r=[zs.d], w=[d_ZF[s]], q="gpsimd")
                        for sub in range(T // 128):
                            tok0 = t0 + sub * 128
                            ts_ = tmst.next()
                            groups = [(0, 256, "copy"), (256, 640, "rope"), (640, 1024, "copy"), (1024, 1408, "silu"),
                                      (1408, 1792, "copy"), (1792, 2176, "copy"), (2176, 2560, "silu")]
                            for gi, (c0, c1, kind) in enumerate(groups):
                                p = pt.next()
                                n = c1 - c0
                                for k in range(8):
                                    PE("matmul", dict(out=p.t[:, :n], lhsT=xn.t[:, k, sub * 128:(sub + 1) * 128], rhs=WIN.t[:, k, c0:c1],
                                        start=(k == 0), stop=(k == 7)), [d_win[k], xn.d], [p.d])
                                if kind == "silu":
                                    A("activation", dict(out=ts_.t[:, c0:c1], in_=p.t[:, :n], func=AF.Silu), [p.d], [ts_.d])
                                elif kind == "rope" and tok0 >= CTXL:
                                    ch = (tok0 - CTXL) // 128
                                    cosb = bc(CST.t[:, C_COS + ch * 24:C_COS + ch * 24 + 24], [128, 8, 24], 1)
                                    sinb = bc(CST.t[:, C_SIN + ch * 24:C_SIN + ch * 24 + 24], [128, 8, 24], 1)
                                    pv = p.t[:, :384].rearrange("p (h m two) -> p h m two", h=8, two=2)
                                    ov = ts_.t[:, c0:c1].rearrange("p (h m two) -> p h m two", h=8, two=2)
                                    ra = tmps.next()
                                    rb = tmps.next()
                                    rav = ra.t[:, 0:192].rearrange("p (h m) -> p h m", h=8)
                                    rbv = rb.t[:, 0:192].rearrange("p (h m) -> p h m", h=8)
                                    V("tensor_tensor", dict(out=rav, in0=pv[:, :, :, 0], in1=cosb, op=ALU.mult), [p.d, CST.d], [ra.d])
                                    V("tensor_tensor", dict(out=rbv, in0=pv[:, :, :, 1], in1=sinb, op=ALU.mult), [p.d, CST.d], [rb.d])
                                    V("tensor_tensor", dict(out=ov[:, :, :, 0], in0=rav, in1=rbv, op=ALU.subtract), [ra.d, rb.d], [ts_.d])
                                    V("tensor_tensor", dict(out=rav, in0=pv[:, :, :, 0], in1=sinb, op=ALU.mult), [p.d, CST.d], [ra.d])
                                    V("tensor_tensor", dict(out=rbv, in0=pv[:, :, :, 1], in1=cosb, op=ALU.mult), [p.d, CST.d], [rb.d])
                                    V("tensor_tensor", dict(out=ov[:, :, :, 1], in0=rav, in1=rbv, op=ALU.add), [ra.d, rb.d], [ts_.d])
                                else:
                                    V("tensor_copy", dict(out=ts_.t[:, c0:c1], in_=p.t[:, :n]), [p.d], [ts_.d])
                            DMA(TM[s][tok0:tok0 + 128, :], ts_.t[:], r=[ts_.d], w=[d_TM[s]], q="gpsimd")
                S.flush()
            if upto in ("ph1", "ph1s"):
                break

            with contextlib.ExitStack() as mx:
                G0 = mk(mx, "G0", [128, 32, 128], BF16)
                WBT = mk(mx, "WBT", [128, 32, 2, 64], BF16)
                WC = mk(mx, "WC", [64, 32, 2, 128], BF16)
                ARI = mk(mx, "ARI", [64, 2, 2, 32])
                REQ = mk(mx, "REQ", [128, 2, 3, 192])
                RDEC = mk(mx, "RDEC", [48, 8])
                GNR = mk(mx, "GNR", [128, 384])
                GNG = mk(mx, "GNG", [128, 384])
                GW = mk(mx, "GW", [32, 2, 192])
                DSK = mk(mx, "DSK", [128, 256])
                GLUB = mk(mx, "GLUB", [128, 2])
                GLUW = mk(mx, "GLUW", [128, 2, 256], BF16)
                pbank = [mkp(mx, "pb%d" % i, [128, 512]) for i in range(6)]
                pOs = Rot([pbank[1], pbank[5]])
                pSs = Rot([pbank[2], pbank[4]])
                pT = mkp(mx, "pT", [128, 8, 128], BF16)
                pT2 = mkp(mx, "pT2", [128, 8, 128], BF16)
                pA, pO, pS, pM, pR, pY = pbank[0], pbank[1], pbank[2], pbank[3], pbank[4], pbank[5]
                MAGIC = 12582912.0
                TWO_PI = 2.0 * math.pi
                C1 = 6.28125
                C2 = TWO_PI - C1
                PI_S = 3.1415925

                with contextlib.ExitStack() as pp:
                    LAM = mk(pp, "LAM", [64, 2, 32])
                    DTL = mk(pp, "DTL", [64, 32])
                    LD = mk(pp, "LD", [64, 2, 32])
                    POW = mk(pp, "POW", [64, 2, 32, 40])
                    WK = [mk(pp, "wk%d" % i, [64, 32, 40]) for i in range(4)]
                    SCT = mk(pp, "SCT", [64, 2, 32, 40])
                    FF = mk(pp, "FF", [64, 2, 32])
                    SM = [mk(pp, "sm%d" % i, [64, 32]) for i in range(4)]
                    BRI = mk(pp, "BRI", [64, 2, 16, 16])
                    CRI = mk(pp, "CRI", [64, 2, 16, 16])
                    BB = mk(pp, "BB", [64, 2, 32, 16])
                    BFt = mk(pp, "BFt", [64, 2, 8, 128])
                    CFt = mk(pp, "CFt", [64, 2, 8, 128])
                    WBm = mk(pp, "WBm", [64, 2, 8, 128])
                    WCm = mk(pp, "WCm", [64, 2, 8, 128])
                    t1 = mk(pp, "t1", [64, 1024])
                    t2 = mk(pp, "t2", [64, 1024])
                    wst2 = Rot([mk(pp, "wst2_%d" % i, [128, 256]) for i in range(2)])
                    LGB = mk(pp, "LGB", [128, 8])
                    NLGB = mk(pp, "NLGB", [128, 8])
                    P1 = mk(pp, "P1", [128, 2, 2, 48])

                    DMA(LAM.t[:], s5lam[l], w=[LAM.d])
                    DMA(DTL.t[:], s5dt[l].partition_broadcast(64), w=[DTL.d])
                    DMA(BRI.t[:], s5B[l], w=[BRI.d])
                    DMA(CRI.t[:], s5C[l], w=[CRI.d])
                    DMA(DSK.t[:], s5_d[l].partition_broadcast(128), w=[DSK.d])
                    DMA(GLUB.t[:], s5_glu_b[l].rearrange("(c p) -> p c", p=128), w=[GLUB.d], allow_slow_non_contiguous=True)
                    DMA(GNR.t[:], ret_gn[l].partition_broadcast(128), w=[GNR.d])
                    DMA(GNG.t[:], gla_norm[l].partition_broadcast(128), w=[GNG.d])
                    DMA(LGB.t[:], ret_ld[l].partition_broadcast(128), w=[LGB.d])
                    G("memset", dict(ap=GW.t[:], constant=0.0), w=[GW.d])
                    DMA(GW.t[0:16, :, :], gla_gw[l].rearrange("d r c -> r d c"), w=[GW.d])
                    DMA(GW.t[16:17, :, :], gla_gb[l].rearrange("(o d) c -> o d c", o=1), w=[GW.d])
                    for kt in range(2):
                        st = wst2.next()
                        DMA(st.t[:], s5_glu_w[l][kt * 128:(kt + 1) * 128, :], w=[st.d])
                        V("tensor_copy", dict(out=GLUW.t[:, kt, :], in_=st.t[:]), [st.d], [GLUW.d])

                    iota48 = CST.t[:, C_IOTA:C_IOTA + 1].broadcast_to([128, 48])
                    V("tensor_scalar", dict(out=P1.t[:, 0, 0, :], in0=iota48, scalar1=1.0, scalar2=None, op0=ALU.add), [CST.d], [P1.d])
                    V("tensor_scalar", dict(out=P1.t[:, 1, 0, :], in0=iota48, scalar1=-1.0, scalar2=128.0, op0=ALU.mult, op1=ALU.add), [CST.d], [P1.d])
                    V("tensor_scalar", dict(out=P1.t[:, 0, 1, :], in0=iota48, scalar1=-1.0, scalar2=127.0, op0=ALU.mult, op1=ALU.add), [CST.d], [P1.d])
                    V("tensor_copy", dict(out=P1.t[:, 1, 1, :], in_=iota48), [CST.d], [P1.d])
                    V("tensor_scalar", dict(out=NLGB.t[:], in0=LGB.t[:], scalar1=-1.0, scalar2=None, op0=ALU.mult), [LGB.d], [NLGB.d])
                    lnk = math.log(48.0 ** -0.5)
                    LNK = mk(pp, "LNK", [128, 1])
                    G("memset", dict(ap=LNK.t[:], constant=lnk), w=[LNK.d])
                    for dd in range(2):
                        for h in range(4):
                            ix = dd * 4 + h
                            hs = slice(h * 48, (h + 1) * 48)
                            A("activation", dict(out=REQ.t[:, dd, 0, hs], in_=P1.t[:, dd, 0, :], func=AF.Exp,
                                                                         scale=LGB.t[:, ix:ix + 1]), [P1.d, LGB.d], [REQ.d])
                            A("activation", dict(out=REQ.t[:, dd, 1, hs], in_=P1.t[:, dd, 0, :], func=AF.Exp,
                                                                         scale=NLGB.t[:, ix:ix + 1], bias=LNK.t[:, 0:1]), [P1.d, NLGB.d, LNK.d], [REQ.d])
                            A("activation", dict(out=REQ.t[:, dd, 2, hs], in_=P1.t[:, dd, 1, :], func=AF.Exp,
                                                                         scale=LGB.t[:, ix:ix + 1], bias=LNK.t[:, 0:1]), [P1.d, LGB.d, LNK.d], [REQ.d])
                    A("activation", dict(out=RDEC.t[:], in_=LGB.t[0:48, :], func=AF.Exp, scale=128.0), [LGB.d], [RDEC.d])

                    A("activation", dict(out=DTL.t[:], in_=DTL.t[:], func=AF.Exp), [DTL.d], [DTL.d])
                    for c in range(2):
                        V("tensor_tensor", dict(out=LD.t[:, c, :], in0=LAM.t[:, c, :], in1=DTL.t[:], op=ALU.mult), [LAM.d, DTL.d], [LD.d])
                    NVv = CST.t[0:64, C_NV:C_NV + 1280].rearrange("p (g t) -> p g t", g=32)
                    V("tensor_tensor", dict(out=WK[0].t[:], in0=NVv, in1=bc(LD.t[:, 0, :], [64, 32, 40], 2), op=ALU.mult), [CST.d, LD.d], [WK[0].d])
                    A("activation", dict(out=WK[0].t[:], in_=WK[0].t[:], func=AF.Exp), [WK[0].d], [WK[0].d])
                    V("tensor_tensor", dict(out=WK[1].t[:], in0=NVv, in1=bc(LD.t[:, 1, :], [64, 32, 40], 2), op=ALU.mult), [CST.d, LD.d], [WK[1].d])
                    for c in range(2):
                        if c == 0:
                            V("tensor_scalar", dict(out=WK[2].t[:], in0=WK[1].t[:], scalar1=math.pi / 2, scalar2=None, op0=ALU.add), [WK[1].d], [WK[2].d])
                            src_ = WK[2]
                        else:
                            src_ = WK[1]
                        V("tensor_scalar", dict(out=WK[3].t[:], in0=src_.t[:], scalar1=1.0 / TWO_PI, scalar2=MAGIC, op0=ALU.mult, op1=ALU.add), [src_.d], [WK[3].d])
                        V("tensor_scalar", dict(out=WK[3].t[:], in0=WK[3].t[:], scalar1=MAGIC, scalar2=None, op0=ALU.subtract), [WK[3].d], [WK[3].d])
                        V("scalar_tensor_tensor", dict(out=src_.t[:], in0=WK[3].t[:], scalar=-C1, in1=src_.t[:], op0=ALU.mult, op1=ALU.add), [WK[3].d, src_.d], [src_.d])
                        V("scalar_tensor_tensor", dict(out=src_.t[:], in0=WK[3].t[:], scalar=-C2, in1=src_.t[:], op0=ALU.mult, op1=ALU.add), [WK[3].d, src_.d], [src_.d])
                        V("tensor_scalar", dict(out=src_.t[:], in0=src_.t[:], scalar1=-PI_S, scalar2=PI_S, op0=ALU.max, op1=ALU.min), [src_.d], [src_.d])
                        A("activation", dict(out=SCT.t[:, c, :, :], in_=src_.t[:], func=AF.Sin), [src_.d], [SCT.d])
                    for c in range(2):
                        V("tensor_tensor", dict(out=POW.t[:, c, :, :], in0=WK[0].t[:], in1=SCT.t[:, c, :, :], op=ALU.mult), [WK[0].d, SCT.d], [POW.d])
                    a_re = POW.t[:, 0, :, 32]
                    a_im = POW.t[:, 1, :, 32]
                    lre = LAM.t[:, 0, :]
                    lim = LAM.t[:, 1, :]
                    s0, s1_, s2_, s3_ = SM
                    pd = [POW.d, LAM.d]
                    V("tensor_scalar", dict(out=s0.t[:], in0=a_re, scalar1=-1.0, scalar2=None, op0=ALU.add), pd, [s0.d])
                    V("tensor_tensor", dict(out=s1_.t[:], in0=lre, in1=lre, op=ALU.mult), pd, [s1_.d])
                    V("tensor_tensor", dict(out=s2_.t[:], in0=lim, in1=lim, op=ALU.mult), pd, [s2_.d])
                    V("tensor_tensor", dict(out=s1_.t[:], in0=s1_.t[:], in1=s2_.t[:], op=ALU.add), [s1_.d, s2_.d], [s1_.d])
                    V("reciprocal", dict(out=s1_.t[:], in_=s1_.t[:]), [s1_.d], [s1_.d])
                    V("tensor_tensor", dict(out=s2_.t[:], in0=s0.t[:], in1=lre, op=ALU.mult), pd + [s0.d], [s2_.d])
                    V("tensor_tensor", dict(out=s3_.t[:], in0=a_im, in1=lim, op=ALU.mult), pd, [s3_.d])
                    V("tensor_tensor", dict(out=s2_.t[:], in0=s2_.t[:], in1=s3_.t[:], op=ALU.add), [s2_.d, s3_.d], [s2_.d])
                    V("tensor_tensor", dict(out=FF.t[:, 0, :], in0=s2_.t[:], in1=s1_.t[:], op=ALU.mult), [s2_.d, s1_.d], [FF.d])
                    V("tensor_tensor", dict(out=s2_.t[:], in0=a_im, in1=lre, op=ALU.mult), pd, [s2_.d])
                    V("tensor_tensor", dict(out=s3_.t[:], in0=s0.t[:], in1=lim, op=ALU.mult), pd + [s0.d], [s3_.d])
                    V("tensor_tensor", dict(out=s2_.t[:], in0=s2_.t[:], in1=s3_.t[:], op=ALU.subtract), [s2_.d, s3_.d], [s2_.d])
                    V("tensor_tensor", dict(out=FF.t[:, 1, :], in0=s2_.t[:], in1=s1_.t[:], op=ALU.mult), [s2_.d, s1_.d], [FF.d])

                    def cmul(outr, outi, ar, ai, br, bi, shape, rd, wd, neg=False):
                        n = 1
                        for x in shape[1:]:
                            n *= x
                        pat = {2: "p (a) -> p a", 3: "p (a b) -> p a b", 4: "p (a b c) -> p a b c"}[len(shape)]
                        kw = {}
                        for nm, sz in zip("abc", shape[1:]):
                            kw[nm] = sz
                        v1 = t1.t[:shape[0], :n].rearrange(pat, **kw)
                        v2 = t2.t[:shape[0], :n].rearrange(pat, **kw)
                        V("tensor_tensor", dict(out=v1, in0=ar, in1=br, op=ALU.mult), rd, [t1.d])
                        V("tensor_tensor", dict(out=v2, in0=ai, in1=bi, op=ALU.mult), rd, [t2.d])
                        V("tensor_tensor", dict(out=outr, in0=v1, in1=v2, op=ALU.subtract), [t1.d, t2.d], wd)
                        V("tensor_tensor", dict(out=v1, in0=ar, in1=bi, op=ALU.mult), rd, [t1.d])
                        V("tensor_tensor", dict(out=v2, in0=ai, in1=br, op=ALU.mult), rd, [t2.d])
                        if neg:
                            V("tensor_tensor", dict(out=v1, in0=v1, in1=v2, op=ALU.add), [t1.d, t2.d], [t1.d])
                            V("tensor_scalar", dict(out=outi, in0=v1, scalar1=-1.0, scalar2=None, op0=ALU.mult), [t1.d], wd)
                        else:
                            V("tensor_tensor", dict(out=outi, in0=v1, in1=v2, op=ALU.add), [t1.d, t2.d], wd)

                    for dd in range(2):
                        gs = slice(dd * 16, (dd + 1) * 16)
                        fr = bc(FF.t[:, 0, gs], [64, 16, 16], 2)
                        fi = bc(FF.t[:, 1, gs], [64, 16, 16], 2)
                        cmul(BB.t[:, 0, gs, :], BB.t[:, 1, gs, :], fr, fi, BRI.t[:, 0, :, :], BRI.t[:, 1, :, :], [64, 16, 16],
                             [FF.d, BRI.d], [BB.d])
                    a8r = POW.t[:, 0, :, 33]
                    a8i = POW.t[:, 1, :, 33]
                    V("tensor_copy", dict(out=ARI.t[:, 0, 0, :], in_=a8r), [POW.d], [ARI.d])
                    V("tensor_copy", dict(out=ARI.t[:, 0, 1, :], in_=a8r), [POW.d], [ARI.d])
                    V("tensor_scalar", dict(out=ARI.t[:, 1, 0, :], in0=a8i, scalar1=-1.0, scalar2=None, op0=ALU.mult), [POW.d], [ARI.d])
                    V("tensor_copy", dict(out=ARI.t[:, 1, 1, :], in_=a8i), [POW.d], [ARI.d])

                    for gb in range(4):
                        dd = gb // 2
                        g0 = (gb % 2) * 8
                        gsl = slice(gb * 8, gb * 8 + 8)
                        sh = [64, 8, 8, 16]
                        v4 = lambda tl, c: tl.t[:, c, :, :].rearrange("p g (j c) -> p g j c", j=8)

                        def pw(c, tsel):
                            return bc(POW.t[:, c, gsl, tsel * 8:tsel * 8 + 8], sh, 3)

                        bbr = bc(BB.t[:, 0, gsl, :], sh, 2)
                        bbi = bc(BB.t[:, 1, gsl, :], sh, 2)
                        cr = bc(CRI.t[:, 0, g0:g0 + 8, :], sh, 2)
                        ci = bc(CRI.t[:, 1, g0:g0 + 8, :], sh, 2)
                        cmul(v4(BFt, 0), v4(BFt, 1), pw(0, 0), pw(1, 0), bbr, bbi, sh, [POW.d, BB.d], [BFt.d])
                        cmul(v4(CFt, 0), v4(CFt, 1), pw(0, 1), pw(1, 1), cr, ci, sh, [POW.d, CRI.d], [CFt.d], neg=True)
                        cmul(v4(WBm, 0), v4(WBm, 1), pw(0, 2), pw(1, 2), bbr, bbi, sh, [POW.d, BB.d], [WBm.d])
                        cmul(v4(WCm, 0), v4(WCm, 1), pw(0, 3), pw(1, 3), cr, ci, sh, [POW.d, CRI.d], [WCm.d], neg=True)
                        for c in range(2):
                            V("tensor_copy", dict(out=WC.t[:, gsl, c, :], in_=WCm.t[:, c, :, :]), [WCm.d], [WC.d])
                        mcol = C_MGF if dd == 0 else C_MGB
                        for gl in range(8):
                            gd = gb * 8 + gl
                            pg = pY if gl % 2 == 0 else pM
                            PE("matmul", dict(out=pg.t[:, 0:128], lhsT=BFt.t[:, 0, gl, :], rhs=CFt.t[:, 0, gl, :], start=True, stop=False), [BFt.d, CFt.d], [pg.d])
                            PE("matmul", dict(out=pg.t[:, 0:128], lhsT=BFt.t[:, 1, gl, :], rhs=CFt.t[:, 1, gl, :], start=False, stop=True), [BFt.d, CFt.d], [pg.d])
                            V("tensor_tensor", dict(out=G0.t[:, gd, :], in0=pg.t[:, 0:128], in1=CST.t[:, mcol:mcol + 128], op=ALU.mult), [pg.d, CST.d], [G0.d])
                            for c in range(2):
                                pr = pR if c == 0 else pS
                                PE("transpose", dict(out=pr.t[:, 0:64], in_=WBm.t[:, c, gl, :], identity=CST.t[0:64, C_ID:C_ID + 64]), [WBm.d, CST.d], [pr.d])
                                A("activation", dict(out=WBT.t[:, gd, c, :], in_=pr.t[:, 0:64], func=AF.Copy), [pr.d], [WBT.d])
                    S.flush()

                U32 = mk(mx, "U32", [128, 8, 256])
                UB = mk(mx, "UB", [128, 16, 128], BF16)
                U8 = mk(mx, "U8", [128, 16, 288], BF16)
                VV = mk(mx, "VV", [64, 289, 2, 32], BF16)
                HS = Rot([mk(mx, "HS%d" % i, [64, 2, 32]) for i in range(2)])
                T1s = mk(mx, "T1s", [64, 2, 32])
                T2s = mk(mx, "T2s", [64, 2, 32])
                YTt = mk(mx, "YTt", [128, 8, 256])
                YYb = mk(mx, "YYb", [128, 8, 256], BF16)
                YYT = mk(mx, "YYT", [128, 2, 8, 128], BF16)
                SGL = mk(mx, "SGL", [128, 4, 128])
                S5O = mk(mx, "S5O", [128, 2, 1024], BF16)
                QKs = Rot([mk(mx, "QK%d" % i, [128, 1152]) for i in range(4)])
                ZAs = Rot([mk(mx, "ZA%d" % i, [32, 128]) for i in range(2)])
                E1 = mk(mx, "E1", [128, 192])
                NL = mk(mx, "NL", [128, 192])
                EQts = Rot([mk(mx, "EQt%d" % i, [128, 3, 192]) for i in range(2)])
                DECts = Rot([mk(mx, "DECt%d" % i, [48, 8]) for i in range(2)])
                QTs = Rot([mk(mx, "QT%d" % i, [128, 192], BF16) for i in range(2)])
                KTs = Rot([mk(mx, "KT%d" % i, [128, 192], BF16) for i in range(2)])
                KHs = Rot([mk(mx, "KH%d" % i, [128, 192], BF16) for i in range(2)])
                VBs = Rot([mk(mx, "VB%d" % i, [128, 384], BF16) for i in range(2)])
                QKTs = Rot([mk(mx, "QKT%d" % i, [48, 8, 128], BF16) for i in range(2)])
                PTs = Rot([mk(mx, "PT%d" % i, [128, 4, 128], BF16) for i in range(2)])
                S32 = [mk(mx, "S32_%d" % i, [48, 4, 96]) for i in range(2)]
                SBs = Rot([mk(mx, "SBf_%d" % i, [48, 4, 96], BF16) for i in range(2)])
                OST = Rot([mk(mx, "OST%d" % i, [128, 384]) for i in range(2)])
                OBt = Rot([mk(mx, "OBt%d" % i, [128, 384]) for i in range(4)])
                OT = mk(mx, "OT", [128, 4, 96])
                SQt = mk(mx, "SQt", [128, 4, 96])
                YN = mk(mx, "YN", [128, 4, 96])
                YB = mk(mx, "YB", [128, 384], BF16)
                ST = [mk(mx, "st%d" % i, [128, 4]) for i in range(5)]
                YFM = [Rot([mk(mx, "YFM%d_%d" % (m_, i), [128, 3, 512], BF16) for i in range(1)]) for m_ in range(2)]

                for za in ZAs.items:
                    G("memset", dict(ap=za.t[:], constant=1.0), w=[za.d])
                G("memset", dict(ap=VV.t[:], constant=0.0), w=[VV.d])

                NT_S5 = [(0, 32), (32, 128), (160, 128)]

                def s5_pre(s):
                    TMv = TM[s].rearrange("(n j) c -> n j c", j=8)
                    for (n0, nn) in NT_S5:
                        DMA(U32.t[:nn], TMv[n0:n0 + nn, :, 0:256], r=[d_TM[s]], w=[U32.d])
                        V("tensor_copy", dict(out=UB.t[:nn].rearrange("p g (j c) -> p j g c", j=8), in_=U32.t[:nn].rearrange("p j (g c) -> p j g c", g=16)), [U32.d], [UB.d])
                        for gh in range(2):
                            pt_ = pT if gh == 0 else pT2
                            for gl in range(8):
                                g = gh * 8 + gl
                                PE("transpose", dict(out=pt_.t[:, gl, :nn], in_=UB.t[:nn, g, :],
                                                                                      identity=IDB.t[:nn, :nn]), [UB.d, IDB.d], [pt_.d])
                            V("tensor_copy", dict(out=U8.t[:, gh * 8:(gh + 1) * 8, n0:n0 + nn], in_=pt_.t[:, :, :nn]), [pt_.d], [U8.d])
                    ev = 0
                    for gd in range(32):
                        g = gd % 16
                        for c in range(2):
                            px = pM if (gd * 2 + c) % 2 == 0 else pR
                            PE("matmul", dict(out=px.t[:64, 0:288], lhsT=WBT.t[:, gd, c, :], rhs=U8.t[:, g, :], start=True, stop=True), [WBT.d, U8.d], [px.d])
                            if gd < 16:
                                outs = [(VV.t[:, 1:289, c, gd], px.t[:64, 0:288])]
                            else:
                                outs = [(VV.t[:, 256:288, c, gd], px.t[:64, 0:32]), (VV.t[:, 0:256, c, gd], px.t[:64, 32:288])]
                            for (o_, i_) in outs:
                                if ev % 2 == 0:
                                    V("tensor_copy", dict(out=o_, in_=i_), [px.d], [VV.d])
                                else:
                                    A("activation", dict(out=o_, in_=i_, func=AF.Copy), [px.d], [VV.d])
                                ev += 1
                    h = HS.next()
                    G("memset", dict(ap=h.t[:], constant=0.0), w=[h.d])
                    PS_VV = VV.t[:, 0, 0, 0:1].ap[0][0]
                    vv_off0 = VV.t[:, 0, 0, 0:1].offset
                    for k in range(288):
                        hn = HS.next()
                        sf = k + 1
                        sb_ = 287 - k
                        hsw = bass.AP(tensor=h.t[:].tensor, offset=h.t[:, 1, :].offset, ap=[list(h.t[:].ap[0]), [-32, 2], [1, 32]])
                        xap = bass.AP(tensor=VV.t[:].tensor, offset=vv_off0 + sf * 64, ap=[[PS_VV, 64], [32, 2], [(sb_ - sf) * 64 + 16, 2], [1, 16]])
                        G("tensor_tensor", dict(out=T1s.t[:], in0=h.t[:], in1=ARI.t[:, 0, :, :], op=ALU.mult), [h.d, ARI.d], [T1s.d])
                        G("tensor_tensor", dict(out=T2s.t[:], in0=hsw, in1=ARI.t[:, 1, :, :], op=ALU.mult), [h.d, ARI.d], [T2s.d])
                        G("tensor_tensor", dict(out=T1s.t[:], in0=T1s.t[:], in1=T2s.t[:], op=ALU.add), [T1s.d, T2s.d], [T1s.d])
                        G("tensor_tensor", dict(out=hn.t[:].rearrange("p c (d g) -> p c d g", d=2), in0=T1s.t[:].rearrange("p c (d g) -> p c d g", d=2), in1=xap, op=ALU.add),
                          [T1s.d, VV.d], [hn.d])
                        G("tensor_copy", dict(out=xap, in_=hn.t[:].rearrange("p c (d g) -> p c d g", d=2)), [hn.d], [VV.d])
                        h = hn

                def s5_post(s):
                    TMv = TM[s].rearrange("(n j) c -> n j c", j=8)
                    YSv = YS5[s].rearrange("(c p) t -> p c t", p=128)
                    for (n0, nn) in NT_S5:
                        bs0 = 257 + n0 if n0 == 0 else n0 - 31
                        DMA(U32.t[:nn], TMv[n0:n0 + nn, :, 0:256], r=[d_TM[s]], w=[U32.d])
                        for g in range(16):
                            mm = [(U8.t[:, g, n0:n0 + nn], G0.t[:, g, :], [U8.d, G0.d]),
                                  (U8.t[:, g, n0:n0 + nn], G0.t[:, 16 + g, :], [U8.d, G0.d]),
                                  (VV.t[:, n0:n0 + nn, 0, g], WC.t[:, g, 0, :], [VV.d, WC.d]),
                                  (VV.t[:, n0:n0 + nn, 1, g], WC.t[:, g, 1, :], [VV.d, WC.d]),
                                  (VV.t[:, bs0:bs0 + nn, 0, 16 + g], WC.t[:, 16 + g, 0, :], [VV.d, WC.d]),
                                  (VV.t[:, bs0:bs0 + nn, 1, 16 + g], WC.t[:, 16 + g, 1, :], [VV.d, WC.d])]
                            for mi, (lh, rh, rd) in enumerate(mm):
                                PE("matmul", dict(out=pY.t[:nn, 0:128], lhsT=lh, rhs=rh, start=(mi == 0), stop=(mi == 5)), rd, [pY.d])
                            o_ = YTt.t[:nn, :, g * 16:(g + 1) * 16]
                            i_ = pY.t[:nn, 0:128].rearrange("p (i c) -> p i c", i=8)
                            if g % 2 == 0:
                                V("tensor_copy", dict(out=o_, in_=i_), [pY.d], [YTt.d])
                            else:
                                A("activation", dict(out=o_, in_=i_, func=AF.Copy), [pY.d], [YTt.d])
                        dsk = bc(DSK.t[:nn, :], [nn, 8, 256], 1)
                        V("tensor_tensor", dict(out=U32.t[:nn], in0=U32.t[:nn], in1=dsk, op=ALU.mult), [U32.d, DSK.d], [U32.d])
                        V("tensor_tensor", dict(out=YTt.t[:nn], in0=YTt.t[:nn], in1=U32.t[:nn], op=ALU.add), [YTt.d, U32.d], [YTt.d])
                        V("tensor_tensor", dict(out=U32.t[:nn], in0=YTt.t[:nn], in1=YTt.t[:nn], op=ALU.mult), [YTt.d], [U32.d])
                        V("tensor_scalar", dict(out=U32.t[:nn], in0=U32.t[:nn], scalar1=0.044715, scalar2=1.0, op0=ALU.mult, op1=ALU.add), [U32.d], [U32.d])
                        V("tensor_tensor", dict(out=U32.t[:nn], in0=U32.t[:nn], in1=YTt.t[:nn], op=ALU.mult), [U32.d, YTt.d], [U32.d])
                        A("activation", dict(out=U32.t[:nn], in_=U32.t[:nn], func=AF.Sigmoid, scale=1.5957691216), [U32.d], [U32.d])
                        V("tensor_tensor", dict(out=YYb.t[:nn], in0=YTt.t[:nn], in1=U32.t[:nn], op=ALU.mult), [YTt.d, U32.d], [YYb.d])
                        for ih in range(2):
                            pt_ = pT if ih == 0 else pT2
                            for il in range(4):
                                i = ih * 4 + il
                                for ct in range(2):
                                    PE("transpose", dict(out=pt_.t[:, il * 2 + ct, :nn], in_=YYb.t[:nn, i, ct * 128:(ct + 1) * 128],
                                                                                                 identity=IDB.t[:nn, :nn]), [YYb.d, IDB.d], [pt_.d])
                            for ct in range(2):
                                i_ = pt_.t[:, :, :nn].rearrange("p (i c) n -> p i c n", c=2)[:, :, ct, :]
                                V("tensor_copy", dict(out=YYT.t[:, ct, ih * 4:(ih + 1) * 4, :nn], in_=i_), [pt_.d], [YYT.d])
                        ntok = nn * 8
                        for co in range(2):
                            for ih in range(2):
                                for kt in range(2):
                                    PE("matmul", dict(out=pO.t[:, 0:4 * nn].rearrange("p (i n) -> p i n", i=4),
                                                                                       lhsT=GLUW.t[:, kt, co * 128:(co + 1) * 128],
                                                                                       rhs=YYT.t[:, kt, ih * 4:(ih + 1) * 4, :nn], start=(kt == 0), stop=(kt == 1)), [GLUW.d, YYT.d], [pO.d])
                                A("activation", dict(out=SGL.t[:, :, :nn], in_=pO.t[:, 0:4 * nn].rearrange("p (i n) -> p i n", i=4),
                                                                      func=AF.Sigmoid, bias=GLUB.t[:, co:co + 1], scale=1.0), [pO.d, GLUB.d], [SGL.d])
                                o_ = S5O.t[:, co, 0:ntok].rearrange("p (n i) -> p i n", i=8)[:, ih * 4:(ih + 1) * 4, :]
                                V("tensor_tensor", dict(out=o_, in0=YYT.t[:, co, ih * 4:(ih + 1) * 4, :nn], in1=SGL.t[:, :, :nn],
                                                                                       op=ALU.mult), [YYT.d, SGL.d], [S5O.d])
                        DMA(YSv[:, :, n0 * 8:n0 * 8 + ntok], S5O.t[:, :, 0:ntok], r=[S5O.d], w=[d_YS5[s]], q="gpsimd")

                def attn(s, kind):
                    gla_ = kind == "gla"
                    Cc = 128
                    NCH = LT // Cc
                    nctx = CTXL // Cc
                    c0 = 1408 if gla_ else 256
                    OBS, d_OBS = (OBG, d_OBG) if gla_ else (OBR, d_OBR)
                    YD, d_YD = (YGLA, d_YGLA) if gla_ else (YRET, d_YRET)
                    GNT = GNG if gla_ else GNR
                    S32t = S32[1 if gla_ else 0]
                    yrot = YFM[1 if gla_ else 0]
                    YDv = YD[s].rearrange("(c p) t -> p c t", p=128)
                    yfh = [None]
                    sbh = [None]

                    def front(c, dd):
                        cx = {"c": c, "dd": dd}
                        tok0 = c * Cc
                        mcol_incl = C_MF if dd == 0 else C_MB
                        mcol_rest = C_MBS if dd == 0 else C_MFS
                        qk = QKs.next()
                        cx["qk"] = qk
                        DMA(qk.t[:Cc, :], TM[s][tok0:tok0 + Cc, c0:c0 + 1152], r=[d_TM[s]], w=[qk.d])
                        if dd == 0:
                            ob = OBt.next()
                            cx["ob"] = ob
                            DMA(ob.t[:Cc, :], OBS[s][tok0:tok0 + Cc, :], r=[d_OBS[s]], w=[ob.d])
                        if gla_:
                            za = ZAs.next()
                            eqt = EQts.next()
                            dect = DECts.next()
                            DMA(za.t[0:16, :Cc], ZF[s][dd * 16:(dd + 1) * 16, tok0:tok0 + Cc], r=[d_ZF[s]], w=[za.d])
                            PE("matmul", dict(out=pM.t[:Cc, 0:192], lhsT=za.t[0:17, :Cc], rhs=GW.t[0:17, dd, :], start=True, stop=True), [za.d, GW.d], [pM.d])
                            A("activation", dict(out=E1.t[:], in_=pM.t[:Cc, 0:192], func=AF.Exp, scale=-1.0), [pM.d], [E1.d])
                            A("activation", dict(out=NL.t[:], in_=E1.t[:], func=AF.Ln, bias=1.0, scale=1.0), [E1.d], [NL.d])
                            PE("matmul", dict(out=pM.t[:Cc, 192:384], lhsT=CST.t[0:Cc, mcol_incl:mcol_incl + Cc], rhs=NL.t[:], start=True, stop=True), [CST.d, NL.d], [pM.d])
                            PE("matmul", dict(out=pM.t[:Cc, 0:192], lhsT=CST.t[0:Cc, mcol_rest:mcol_rest + Cc], rhs=NL.t[:], start=True, stop=True), [CST.d, NL.d], [pM.d])
                            for h in range(4):
                                PE("matmul", dict(out=pM.t[:48, 384 + 2 * h:386 + 2 * h], lhsT=NL.t[:, h * 48:(h + 1) * 48], rhs=ONES32.t[0:Cc, :],
                                                  start=True, stop=True), [NL.d, ONES32.d], [pM.d])
                            A("activation", dict(out=eqt.t[:, 0, :], in_=pM.t[:Cc, 192:384], func=AF.Exp, scale=-1.0 / 16, bias=LNKm.t[0:Cc, 0:1]), [pM.d, LNKm.d], [eqt.d])
                            A("activation", dict(out=eqt.t[:, 1, :], in_=pM.t[:Cc, 192:384], func=AF.Exp, scale=1.0 / 16), [pM.d], [eqt.d])
                            A("activation", dict(out=eqt.t[:, 2, :], in_=pM.t[:Cc, 0:192], func=AF.Exp, scale=-1.0 / 16), [pM.d], [eqt.d])
                            A("activation", dict(out=dect.t[:], in_=pM.t[:48, 384:392], func=AF.Exp, scale=-1.0 / 16), [pM.d], [dect.d])
                            eq, ek, ekh = eqt.t[:, 0, :], eqt.t[:, 1, :], eqt.t[:, 2, :]
                            edep = [eqt.d]
                            cx["dec"] = [dect.t[:, 2 * h:2 * h + 1] for h in range(4)]
                            cx["ddep"] = [dect.d]
                        else:
                            eq, ek, ekh = REQ.t[:, dd, 0, :], REQ.t[:, dd, 1, :], REQ.t[:, dd, 2, :]
                            edep = [REQ.d]
                            cx["dec"] = [RDEC.t[:, dd * 4 + h:dd * 4 + h + 1] for h in range(4)]
                            cx["ddep"] = [RDEC.d]
                        qt = QTs.next()
                        kt_ = KTs.next()
                        kh = KHs.next()
                        vb = VBs.next()
                        qkt = QKTs.next()
                        ptl = PTs.next()
                        po = pOs.next()
                        ps_ = pSs.next()
                        cx.update(qkt=qkt, po=po, ps=ps_)
                        V("tensor_tensor", dict(out=qt.t[:Cc, :], in0=qk.t[:Cc, 0:192], in1=eq, op=ALU.mult), [qk.d] + edep, [qt.d])
                        V("tensor_tensor", dict(out=kt_.t[:Cc, :], in0=qk.t[:Cc, 192:384], in1=ek, op=ALU.mult), [qk.d] + edep, [kt_.d])
                        V("tensor_tensor", dict(out=kh.t[:Cc, :], in0=qk.t[:Cc, 192:384], in1=ekh, op=ALU.mult), [qk.d] + edep, [kh.d])
                        A("activation", dict(out=vb.t[:Cc, :], in_=qk.t[:Cc, 384:768], func=AF.Copy), [qk.d], [vb.d])
                        for h in range(4):
                            PE("transpose", dict(out=pT.t[:48, h, :Cc], in_=qt.t[:Cc, h * 48:(h + 1) * 48], identity=IDB.t[:Cc, :Cc]), [qt.d, IDB.d], [pT.d])
                            PE("transpose", dict(out=pT.t[:48, 4 + h, :Cc], in_=kt_.t[:Cc, h * 48:(h + 1) * 48], identity=IDB.t[:Cc, :Cc]), [kt_.d, IDB.d], [pT.d])
                        V("tensor_copy", dict(out=qkt.t[:, :, :Cc], in_=pT.t[:48, :, :Cc]), [pT.d], [qkt.d])
                        for h in range(4):
                            PE("matmul", dict(out=pA.t[:Cc, h * 128:h * 128 + Cc], lhsT=qkt.t[:, 4 + h, :Cc], rhs=qkt.t[:, h, :Cc], start=True, stop=True), [qkt.d], [pA.d])
                        V("tensor_tensor", dict(out=ptl.t[:Cc, :, :Cc], in0=pA.t[:Cc, :].rearrange("p (h i) -> p h i", h=4)[:, :, :Cc],
                                                in1=MASKB.t[:Cc, dd, :, :Cc], op=ALU.mult), [pA.d, MASKB.d], [ptl.d])
                        for h in range(4):
                            PE("matmul", dict(out=ps_.t[:48, h * 96:(h + 1) * 96], lhsT=kh.t[:Cc, h * 48:(h + 1) * 48], rhs=vb.t[:Cc, h * 96:(h + 1) * 96],
                                              start=True, stop=True), [kh.d, vb.d], [ps_.d])
                        cx["ptl"] = ptl
                        cx["vb"] = vb
                        return cx

                    def back(cx):
                        c, dd = cx["c"], cx["dd"]
                        tok0 = c * Cc
                        qk, qkt, po, ps_, ptl, vb = cx["qk"], cx["qkt"], cx["po"], cx["ps"], cx["ptl"], cx["vb"]
                        sbt = sbh[0]
                        for h in range(4):
                            PE("matmul", dict(out=po.t[:Cc, h * 96:(h + 1) * 96], lhsT=ptl.t[:Cc, h, :Cc], rhs=vb.t[:Cc, h * 96:(h + 1) * 96],
                                              start=True, stop=False), [ptl.d, vb.d], [po.d])
                            PE("matmul", dict(out=po.t[:Cc, h * 96:(h + 1) * 96], lhsT=qkt.t[:, h, :Cc], rhs=sbt.t[:, h, :], start=False, stop=True),
                               [qkt.d, sbt.d], [po.d])
                        for h in range(4):
                            V("scalar_tensor_tensor", dict(out=S32t.t[:, h, :], in0=S32t.t[:, h, :], scalar=cx["dec"][h],
                                                           in1=ps_.t[:48, h * 96:(h + 1) * 96], op0=ALU.mult, op1=ALU.add),
                              [S32t.d, ps_.d] + cx["ddep"], [S32t.d])
                        sbn = SBs.next()
                        A("activation", dict(out=sbn.t[:], in_=S32t.t[:], func=AF.Copy), [S32t.d], [sbn.d])
                        sbh[0] = sbn
                        if dd == 1:
                            os_ = OST.next()
                            A("activation", dict(out=os_.t[:Cc, :], in_=po.t[:Cc, 0:384], func=AF.Copy), [po.d], [os_.d])
                            DMA(OBS[s][tok0:tok0 + Cc, :], os_.t[:Cc, :], r=[os_.d], w=[d_OBS[s]], q="scalar")
                            return
                        if (l == depth - 1) and tok0 < CTXL:
                            if tok0 % 512 == 0:
                                yfh[0] = yrot.next()
                            return
                        ob = cx["ob"]
                        OTf = OT.t[:Cc].rearrange("p h v -> p (h v)")
                        V("tensor_tensor", dict(out=OTf, in0=po.t[:Cc, 0:384], in1=ob.t[:Cc, :], op=ALU.add), [po.d, ob.d], [OT.d])
                        sS1, sS2, sM, sV, sR = ST
                        A("activation", dict(out=SQt.t[:Cc], in_=OT.t[:Cc], func=AF.Square), [OT.d], [SQt.d])
                        V("tensor_reduce", dict(out=sS2.t[:Cc, :], in_=SQt.t[:Cc], axis=AX.X, op=ALU.add), [SQt.d], [sS2.d])
                        if not gla_:
                            V("tensor_reduce", dict(out=sS1.t[:Cc, :], in_=OT.t[:Cc], axis=AX.X, op=ALU.add), [OT.d], [sS1.d])
                            V("tensor_scalar", dict(out=sM.t[:Cc, :], in0=sS1.t[:Cc, :], scalar1=1.0 / 96, scalar2=None, op0=ALU.mult), [sS1.d], [sM.d])
                            V("tensor_tensor", dict(out=sV.t[:Cc, :], in0=sM.t[:Cc, :], in1=sM.t[:Cc, :], op=ALU.mult), [sM.d], [sV.d])
                            V("scalar_tensor_tensor", dict(out=sV.t[:Cc, :], in0=sS2.t[:Cc, :], scalar=1.0 / 96, in1=sV.t[:Cc, :], op0=ALU.mult, op1=ALU.subtract),
                              [sS2.d, sV.d], [sV.d])
                        else:
                            V("tensor_scalar", dict(out=sV.t[:Cc, :], in0=sS2.t[:Cc, :], scalar1=1.0 / 96, scalar2=None, op0=ALU.mult), [sS2.d], [sV.d])
                        A("activation", dict(out=sR.t[:Cc, :], in_=sV.t[:Cc, :], func=AF.Sqrt, bias=EPSm.t[:Cc, 0:1], scale=1.0), [sV.d, EPSm.d], [sR.d])
                        V("reciprocal", dict(out=sR.t[:Cc, :], in_=sR.t[:Cc, :]), [sR.d], [sR.d])
                        for h in range(4):
                            if gla_:
                                V("tensor_scalar", dict(out=YN.t[:Cc, h, :], in0=OT.t[:Cc, h, :], scalar1=sR.t[:Cc, h:h + 1], scalar2=None, op0=ALU.mult),
                                  [OT.d, sR.d], [YN.d])
                            else:
                                V("tensor_scalar", dict(out=YN.t[:Cc, h, :], in0=OT.t[:Cc, h, :], scalar1=sM.t[:Cc, h:h + 1], scalar2=sR.t[:Cc, h:h + 1],
                                                        op0=ALU.subtract, op1=ALU.mult), [OT.d, sM.d, sR.d], [YN.d])
                        YNf = YN.t[:Cc].rearrange("p h v -> p (h v)")
                        V("tensor_tensor", dict(out=YNf, in0=YNf, in1=GNT.t[:Cc, :], op=ALU.mult), [YN.d, GNT.d], [YN.d])
                        V("tensor_tensor", dict(out=YB.t[:Cc, :], in0=YNf, in1=qk.t[:Cc, 768:1152], op=ALU.mult), [YN.d, qk.d], [YB.d])
                        seg = tok0 // 512
                        off = tok0 % 512
                        if off == 0:
                            yfh[0] = yrot.next()
                        yf = yfh[0]
                        for t in range(3):
                            PE("transpose", dict(out=pT2.t[:, t, :Cc], in_=YB.t[:Cc, t * 128:(t + 1) * 128], identity=IDB.t[:Cc, :Cc]), [YB.d, IDB.d], [pT2.d])
                        A("activation", dict(out=yf.t[:, :, off:off + Cc], in_=pT2.t[:, 0:3, :Cc], func=AF.Copy), [pT2.d], [yf.d])
                        if off + Cc == 512 or tok0 + Cc == LT:
                            w_ = off + Cc
                            DMA(YDv[:, :, seg * 512:seg * 512 + w_], yf.t[:, :, 0:w_], r=[yf.d], w=[d_YD[s]], q="scalar")

                    for dd in (1, 0):
                        order = list(range(NCH)) if dd == 0 else list(range(nctx - 1, -1, -1)) + list(range(NCH - 1, nctx - 1, -1))
                        V("memset", dict(ap=S32t.t[:], constant=0.0), w=[S32t.d])
                        sb0 = SBs.next()
                        V("memset", dict(ap=sb0.t[:], constant=0.0), w=[sb0.d])
                        sbh[0] = sb0
                        for c in order:
                            cx = front(c, dd)
                            yield
                            back(cx)
                            yield

                LNKm = mk(mx, "LNKm", [128, 1])
                EPSm = mk(mx, "EPSm", [128, 1])
                G("memset", dict(ap=LNKm.t[:], constant=math.log(48.0 ** -0.5)), w=[LNKm.d])
                G("memset", dict(ap=EPSm.t[:], constant=EPS), w=[EPSm.d])
                for s in range(nseq):
                    if FLAGS.get("s5", True):
                        s5_pre(s)
                    gens = []
                    if FLAGS.get("ret", True):
                        gens.append(attn(s, "ret"))
                    if FLAGS.get("gla", True):
                        gens.append(attn(s, "gla"))
                    while gens:
                        for g_ in list(gens):
                            try:
                                next(g_)
                            except StopIteration:
                                gens.remove(g_)
                    if FLAGS.get("s5", True):
                        s5_post(s)
                S.flush()
            if upto == "mix":
                break


            last = (l == depth - 1)
            HH = HHs
            with contextlib.ExitStack() as ph:
                WBS = mk(ph, "WBS", [128, 2, D], BF16)
                WBR = mk(ph, "WBR", [128, 3, D], BF16)
                WBG = mk(ph, "WBG", [128, 3, D], BF16)
                WO = mk(ph, "WO", [128, 8, D], BF16)
                dws = [Dep() for _ in range(2)]
                dwr = [Dep() for _ in range(3)]
                dwg = [Dep() for _ in range(3)]
                dwo = [Dep() for _ in range(8)]
                stg = Rot([mk(ph, "stg%d" % i, [128, 1024]) for i in range(2)])
                engs = ["gpsimd", "vector", "scalar"]
                load_weight(WBS, dws, w_br_s5[l], 2, D, stg, engs, 1024)
                load_weight(WBR, dwr, w_br_ret[l], 3, D, stg, engs, 1024)
                load_weight(WBG, dwg, w_br_gla[l], 3, D, stg, engs, 1024)
                load_weight(WO, dwo, w_out[l], 8, D, stg, engs, 1024)
                xts = Rot([mk(ph, "xt%d" % i, [128, 8, 512]) for i in range(2)])
                yss = Rot([mk(ph, "ys%d" % i, [128, 8, 512], BF16) for i in range(2)])
                gts = Rot([mk(ph, "gt%d" % i, [128, 24, 512], BF16) for i in range(2)])
                mTs = Rot([mk(ph, "mT%d" % i, [128, 8, 512], BF16) for i in range(2)])
                m1s = Rot([mk(ph, "m1_%d" % i, [128, 512], BF16) for i in range(2)])
                m2s = Rot([mk(ph, "m2_%d" % i, [128, 512], BF16) for i in range(3)])
                pp4 = Rot([mkp(ph, "pq%d" % i, [128, 512]) for i in range(7)])
                for s in range(nseq):
                    for (t0, T) in TOK_TILES:
                        if last and t0 < CTXL:
                            continue
                        col = 4 if t0 < CTXL else s
                        xt = xts.next()
                        ys = yss.next()
                        gt = gts.next()
                        mT = mTs.next()
                        DMA(xt.t[:, :, :T], XIN[s][:, t0:t0 + T].rearrange("(k p) t -> p k t", p=128), r=[d_XIN[s]], w=[xt.d])
                        DMA(ys.t[:, 0:2, :T], YS5[s][:, t0:t0 + T].rearrange("(c p) t -> p c t", p=128), r=[d_YS5[s]], w=[ys.d])
                        DMA(ys.t[:, 2:5, :T], YRET[s][:, t0:t0 + T].rearrange("(c p) t -> p c t", p=128), r=[d_YRET[s]], w=[ys.d])
                        DMA(ys.t[:, 5:8, :T], YGLA[s][:, t0:t0 + T].rearrange("(c p) t -> p c t", p=128), r=[d_YGLA[s]], w=[ys.d])
                        DMA(gt.t[:, :, :T], GATE[s][:, t0:t0 + T].rearrange("(m p) t -> p m t", p=128), r=[d_GATE[s]], w=[gt.d])
                        for dt in range(8):
                            cs = slice(dt * 128, (dt + 1) * 128)
                            m1 = m1s.next()
                            specs = [(WBS, dws, 0, 2, 0), (WBR, dwr, 2, 3, 8), (WBG, dwg, 5, 3, 16)]
                            for bi, (Wt, dw, y0, nk, g0_) in enumerate(specs):
                                p = pp4.next()
                                for kt in range(nk):
                                    PE("matmul", dict(out=p.t[:, :T], lhsT=Wt.t[:, kt, cs], rhs=ys.t[:, y0 + kt, :T], start=(kt == 0), stop=(kt == nk - 1)),
                                       [dw[kt], ys.d], [p.d])
                                if bi == 0:
                                    V("tensor_tensor", dict(out=m1.t[:, :T], in0=p.t[:, :T], in1=gt.t[:, g0_ + dt, :T], op=ALU.mult), [p.d, gt.d], [m1.d])
                                else:
                                    m2 = m2s.next()
                                    V("tensor_tensor", dict(out=m2.t[:, :T], in0=p.t[:, :T], in1=gt.t[:, g0_ + dt, :T], op=ALU.mult), [p.d, gt.d], [m2.d])
                                    if bi == 1:
                                        V("tensor_tensor", dict(out=m1.t[:, :T], in0=m1.t[:, :T], in1=m2.t[:, :T], op=ALU.add), [m1.d, m2.d], [m1.d])
                                    else:
                                        V("tensor_tensor", dict(out=mT.t[:, dt, :T], in0=m1.t[:, :T], in1=m2.t[:, :T], op=ALU.add), [m1.d, m2.d], [mT.d])
                        for dt in range(8):
                            cs = slice(dt * 128, (dt + 1) * 128)
                            p = pp4.next()
                            for k in range(8):
                                PE("matmul", dict(out=p.t[:, :T], lhsT=WO.t[:, k, cs], rhs=mT.t[:, k, :T], start=(k == 0), stop=(k == 7)), [dwo[k], mT.d], [p.d])
                            V("scalar_tensor_tensor", dict(out=xt.t[:, dt, :T], in0=p.t[:, :T], scalar=MOD.t[:, l, 16 + dt, col:col + 1], in1=xt.t[:, dt, :T],
                                                           op0=ALU.mult, op1=ALU.add), [p.d, MOD.d, xt.d], [xt.d])
                        DMA(XMID[s][:, t0:t0 + T].rearrange("(k p) t -> p k t", p=128), xt.t[:, :, :T], r=[xt.d], w=[d_XMID[s]], q="gpsimd")
                S.flush()
            if upto == "ph4a":
                break

            with contextlib.ExitStack() as ph:
                WF1 = mk(ph, "WF1", [128, 8, 2 * FFH], BF16)
                dw1 = [Dep() for _ in range(8)]
                stg = Rot([mk(ph, "stg%d" % i, [128, 512]) for i in range(2)])
                engs = ["gpsimd", "vector", "scalar"]
                load_weight(WF1, dw1, w_ffn_in[l], 8, 2 * FFH, stg, engs, 512)
                xts = Rot([mk(ph, "xt%d" % i, [128, 8, 512]) for i in range(2)])
                sq = mk(ph, "sq", [128, 8, 512], BF16)
                xn = mk(ph, "xn", [128, 8, 512], BF16)
                rs = mk(ph, "rs", [128, 512])
                tmps = Rot([mk(ph, "tmp%d" % i, [128, 512]) for i in range(2)])
                hst = Rot([mk(ph, "hst%d" % i, [128, 11, 512], BF16) for i in range(2)])
                sas = Rot([mk(ph, "sa%d" % i, [128, 512]) for i in range(2)])
                pms = mkp(ph, "pms", [128, 512])
                pp4 = Rot([mkp(ph, "pq%d" % i, [128, 512]) for i in range(6)])
                for s in range(nseq):
                    for (t0, T) in TOK_TILES:
                        if last and t0 < CTXL:
                            continue
                        col = 4 if t0 < CTXL else s
                        xt = xts.next()
                        DMA(xt.t[:, :, :T], XMID[s][:, t0:t0 + T].rearrange("(k p) t -> p k t", p=128), r=[d_XMID[s]], w=[xt.d])
                        norm_mod(xt, T, 1, 24, l, col, sq, pms, rs, tmps, xn)
                        hs_ = None
                        for ft in range(22):
                            if ft % 11 == 0:
                                hs_ = hst.next()
                            pa = pp4.next()
                            pb = pp4.next()
                            for k in range(8):
                                PE("matmul", dict(out=pa.t[:, :T], lhsT=WF1.t[:, k, ft * 128:(ft + 1) * 128], rhs=xn.t[:, k, :T], start=(k == 0), stop=(k == 7)),
                                   [dw1[k], xn.d], [pa.d])
                            for k in range(8):
                                PE("matmul", dict(out=pb.t[:, :T], lhsT=WF1.t[:, k, FFH + ft * 128:FFH + (ft + 1) * 128], rhs=xn.t[:, k, :T], start=(k == 0), stop=(k == 7)),
                                   [dw1[k], xn.d], [pb.d])
                            sa = sas.next()
                            A("activation", dict(out=sa.t[:, :T], in_=pa.t[:, :T], func=AF.Silu), [pa.d], [sa.d])
                            V("tensor_tensor", dict(out=hs_.t[:, ft % 11, :T], in0=pb.t[:, :T], in1=sa.t[:, :T], op=ALU.mult), [pb.d, sa.d], [hs_.d])
                            if ft % 11 == 10:
                                f0 = (ft // 11) * 11
                                DMA(HH[s][f0 * 128:(f0 + 11) * 128, t0:t0 + T].rearrange("(f p) t -> p f t", p=128), hs_.t[:, :, :T], r=[hs_.d], w=[d_HH[s]], q="gpsimd")
                S.flush()

            with contextlib.ExitStack() as ph:
                WF2 = mk(ph, "WF2", [128, 22, D], BF16)
                dw2 = [Dep() for _ in range(22)]
                stg = Rot([mk(ph, "stg%d" % i, [128, 1024]) for i in range(2)])
                engs = ["gpsimd", "vector"]
                load_weight(WF2, dw2, w_ffn_out[l], 22, D, stg, engs, 1024)
                xts = Rot([mk(ph, "xt%d" % i, [128, 8, 512]) for i in range(2)])
                hhs = Rot([mk(ph, "hh%d" % i, [128, 22, 512], BF16) for i in range(2)])
                sq = mk(ph, "sq", [128, 8, 512], BF16)
                rs = mk(ph, "rs", [128, 512])
                pms = mkp(ph, "pms", [128, 512])
                pp4 = Rot([mkp(ph, "pq%d" % i, [128, 512]) for i in range(6)])
                EPSf = mk(ph, "EPSf", [128, 1])
                G("memset", dict(ap=EPSf.t[:], constant=EPS), w=[EPSf.d])
                for s in range(nseq):
                    for (t0, T) in TOK_TILES:
                        if last and t0 < CTXL:
                            continue
                        col = 4 if t0 < CTXL else s
                        xt = xts.next()
                        hh = hhs.next()
                        DMA(xt.t[:, :, :T], XMID[s][:, t0:t0 + T].rearrange("(k p) t -> p k t", p=128), r=[d_XMID[s]], w=[xt.d])
                        DMA(hh.t[:, :, :T], HH[s][:, t0:t0 + T].rearrange("(f p) t -> p f t", p=128), r=[d_HH[s]], w=[hh.d])
                        for dt in range(8):
                            cs = slice(dt * 128, (dt + 1) * 128)
                            p = pp4.next()
                            for ft in range(22):
                                PE("matmul", dict(out=p.t[:, :T], lhsT=WF2.t[:, ft, cs], rhs=hh.t[:, ft, :T], start=(ft == 0), stop=(ft == 21)), [dw2[ft], hh.d], [p.d])
                            V("scalar_tensor_tensor", dict(out=xt.t[:, dt, :T], in0=p.t[:, :T], scalar=MOD.t[:, l, 40 + dt, col:col + 1], in1=xt.t[:, dt, :T],
                                                           op0=ALU.mult, op1=ALU.add), [p.d, MOD.d, xt.d], [xt.d])
                        if not last:
                            DMA(X1[s][:, t0:t0 + T].rearrange("(k p) t -> p k t", p=128), xt.t[:, :, :T], r=[xt.d], w=[d_X1[s]], q="gpsimd")
                        else:
                            V("tensor_tensor", dict(out=sq.t[:, :, :T], in0=xt.t[:, :, :T], in1=xt.t[:, :, :T], op=ALU.mult), [xt.d], [sq.d])
                            for k in range(8):
                                PE("matmul", dict(out=pms.t[:, :T], lhsT=ONESB.t[:], rhs=sq.t[:, k, :T], start=(k == 0), stop=(k == 7)), [ONESB.d, sq.d], [pms.d])
                            A("activation", dict(out=rs.t[:, :T], in_=pms.t[:, :T], func=AF.Ln, bias=EPSf.t[:, 0:1], scale=1.0), [pms.d, EPSf.d], [rs.d])
                            A("activation", dict(out=rs.t[:, :T], in_=rs.t[:, :T], func=AF.Exp, scale=-0.5), [rs.d], [rs.d])
                            for k in range(8):
                                V("scalar_tensor_tensor", dict(out=xt.t[:, k, :T], in0=xt.t[:, k, :T], scalar=NRM.t[:, 2, 0, k:k + 1], in1=rs.t[:, :T],
                                                               op0=ALU.mult, op1=ALU.mult), [xt.d, NRM.d, rs.d], [xt.d])
                            DMA(outT[s][:, t0 - CTXL:t0 - CTXL + T].rearrange("(k p) t -> p k t", p=128), xt.t[:, :, :T], r=[xt.d], w=[d_OUT], q="gpsimd")
                S.flush()

    return nc


def _consts():
    c = np.zeros((128, NCST), np.float32)
    j = np.arange(128)[:, None]
    i = np.arange(128)[None, :]
    c[:, C_ID:C_ID + 128] = (j == i)
    c[:, C_MF:C_MF + 128] = (j <= i)
    c[:, C_MB:C_MB + 128] = (j >= i)
    c[:, C_MFS:C_MFS + 128] = (j < i)
    c[:, C_MBS:C_MBS + 128] = (j > i)
    c[:, C_IOTA] = np.arange(128)
    nf = 12
    inv = (1.0 / (np.float32(10000.0) ** (np.arange(nf, dtype=np.float32) / np.float32(nf)))).astype(np.float32)
    t = (np.arange(16)[None, :] * 128 + np.arange(128)[:, None]).astype(np.float32)
    r = np.floor(t / 64.0).astype(np.float32)
    cc = (t - r * 64.0).astype(np.float32)
    ang = np.concatenate([r[:, :, None] * inv[None, None, :], cc[:, :, None] * inv[None, None, :]], -1).astype(np.float32)
    c[:, C_COS:C_COS + 384] = np.cos(ang).reshape(128, 384)
    c[:, C_SIN:C_SIN + 384] = np.sin(ang).reshape(128, 384)
    nv = np.zeros((32, 5, 8), np.float32)
    idx = np.arange(8, dtype=np.float32)
    for gd in range(32):
        if gd < 16:
            nv[gd, 0] = -idx
            nv[gd, 1] = idx
            nv[gd, 2] = 7 - idx
            nv[gd, 3] = idx + 1
        else:
            nv[gd, 0] = idx - 7
            nv[gd, 1] = 7 - idx
            nv[gd, 2] = idx
            nv[gd, 3] = 8 - idx
        nv[gd, 4, 0] = 1
        nv[gd, 4, 1] = 8
    c[:, C_NV:C_NV + 1280] = nv.reshape(1, 1280)
    jj = (np.arange(128) // 16)[:, None]
    ii = (np.arange(128) // 16)[None, :]
    c[:, C_MGF:C_MGF + 128] = (ii >= jj)
    c[:, C_MGB:C_MGB + 128] = (ii <= jj)
    return c


def host_prep(inputs, core, nseq=NSEQ):
    f = lambda a: np.ascontiguousarray(a, dtype=np.float32)
    b0 = core * nseq
    x = inputs["x"][b0:b0 + nseq]
    ctx = inputs["ctx"][b0:b0 + nseq]
    xt = np.concatenate([ctx, x], axis=1).transpose(0, 2, 1)
    ct = np.zeros((D, 8), np.float32)
    ct[:, :nseq] = inputs["c"][b0:b0 + nseq].T
    ct[:, 4] = inputs["c_ctx"]
    m = {
        "xT": f(xt), "cT": ct, "cst": _consts(),
        "s5lam": f(np.stack([inputs["s5_lam_re"], inputs["s5_lam_im"]], 1).transpose(0, 4, 1, 2, 3).reshape(DEPTH, 64, 2, 32)),
        "s5dt": f(inputs["s5_log_dt"].reshape(DEPTH, 32)),
        "s5B": f(np.stack([inputs["s5_b_re"], inputs["s5_b_im"]], 1).transpose(0, 3, 1, 2, 4)),
        "s5C": f(np.stack([inputs["s5_c_re"], inputs["s5_c_im"]], 1).transpose(0, 4, 1, 2, 3)),
        "ret_ld": f(inputs["ret_log_decay"].reshape(DEPTH, 8)),
        "gla_gw": f(inputs["gla_gate_w"]), "gla_gb": f(inputs["gla_gate_b"]),
    }
    for k in ["w_mod", "b_mod", "norm_mix", "norm_ffn", "w_in", "s5_d", "s5_glu_w", "s5_glu_b", "ret_gn", "gla_norm",
              "w_br_s5", "w_br_ret", "w_br_gla", "w_out", "w_ffn_in", "w_ffn_out", "norm_final"]:
        m[k] = f(inputs[k])
    return m


_NC_CACHE = {}


def kernel(**inputs):
    if "nc" not in _NC_CACHE:
        _NC_CACHE["nc"] = build()
    nc = _NC_CACHE["nc"]
    in_maps = [host_prep(inputs, c) for c in range(8)]
    res = run_bass_kernel_spmd(nc, in_maps, core_ids=list(range(8)))
    outs = [np.asarray(r["outT"]).transpose(0, 2, 1) for r in res.results]
    return np.ascontiguousarray(np.concatenate(outs, axis=0), dtype=np.float32)
```
